# Optimizing a Trainium2 kernel written in Bass

```python
import math
import jax, jax.numpy as jnp
from jax import lax
import numpy as np

D_MODEL = 1024
BATCH = 2
SEQ = 8192
DEPTH = 2
DEC_BATCH = 128
DEC_SEQ = 4
PAST_LEN = 2048
PAGE_SIZE = 128

N_EVEN = (DEPTH + 1) // 2
N_ODD = DEPTH // 2
D_A = D_MODEL // 2
N_BLK_A = 8
CONV_W = 4
RG_C = 8.0
DH_B = 64
H_B = (D_MODEL // 2) // DH_B
W_B = H_B * DH_B
DIL_PATTERNS = ((128, 1), (512, 4), (2048, 16))
WIN_MAX = 2048
DH_C = 64
H_C = (D_MODEL // 2) // (2 * DH_C)
W_C = H_C * 2 * DH_C
DK_D = 128
DV_D = 128
H_D = (D_MODEL // 2) // DV_D
W_DK = H_D * DK_D
W_D = H_D * DV_D
CHUNK_D = 16
N_BUCKETS = 32
T5_MAX_DIST = 2048
N_BIAS = H_B
Q_BLK = 128
IN_AB = 2 * D_A + 4 * W_B
OUT_AB = D_A + W_B
IN_CD = 4 * W_C + 2 * W_DK + 2 * W_D
OUT_CD = W_C + W_D
EPS = 1e-6
NEG = -1e30

kernel_name = 'hawk_longnet_diff_hgrn2_hybrid_step'


def _split(x, sizes):
    idx, acc = [], 0
    for s in sizes[:-1]:
        acc += s
        idx.append(acc)
    return jnp.split(x, idx, axis=-1)


def _rmsnorm(x, g):
    xf = x.astype(jnp.float32)
    y = xf * lax.rsqrt(jnp.mean(xf * xf, axis=-1, keepdims=True) + EPS)
    return (y * g.astype(jnp.float32)).astype(x.dtype)


def _t5_bucket(dist):
    n = jnp.maximum(dist, 0)
    exact = N_BUCKETS // 2
    nf = jnp.maximum(n, 1).astype(jnp.float32)
    large = exact + (jnp.log(nf / exact) / math.log(T5_MAX_DIST / exact) * (N_BUCKETS - exact)).astype(jnp.int32)
    large = jnp.minimum(large, N_BUCKETS - 1)
    return jnp.where(n < exact, n, large)


def _causal_conv(x, buf, w, b):
    t = x.shape[1]
    xp = jnp.concatenate([buf.astype(x.dtype), x], axis=1)
    y = xp[:, 0:t] * w[0]
    for j in range(1, CONV_W):
        y = y + xp[:, j:j + t] * w[j]
    return y + b, xp[:, t:]


def _lin_combine(e1, e2):
    a1, b1 = e1
    a2, b2 = e2
    return a1 * a2, a2 * b1 + b2


def _rglru(xc, h0, w_r, b_r, w_i, b_i, lam):
    b, t, c = xc.shape
    xb = xc.reshape(b, t, N_BLK_A, c // N_BLK_A)
    r = jax.nn.sigmoid((jnp.einsum('btgi,gij->btgj', xb, w_r) + b_r).astype(jnp.float32)).reshape(b, t, c)
    gi = jax.nn.sigmoid((jnp.einsum('btgi,gij->btgj', xb, w_i) + b_i).astype(jnp.float32)).reshape(b, t, c)
    log_a = -RG_C * r * jax.nn.softplus(-lam.astype(jnp.float32))
    a = jnp.exp(log_a)
    u = jnp.sqrt(-jnp.expm1(2.0 * log_a)) * (gi * xc.astype(jnp.float32))
    a_cum, h_in = lax.associative_scan(_lin_combine, (a, u), axis=1)
    h = a_cum * h0.astype(jnp.float32)[:, None, :] + h_in
    return h, h[:, -1]


def _dilated_band_prompt(q, k, v, bias_j, dil, n_back):
    b, s, h, dh = q.shape
    blk = n_back
    unit = dil * blk
    sp = -(-s // unit) * unit
    nb = sp // unit
    pad = ((0, 0), (0, sp - s), (0, 0), (0, 0))

    def split(z):
        return jnp.pad(z, pad).reshape(b, nb, blk, dil, h, dh)

    def with_prev(z):
        prev = jnp.pad(z, ((0, 0), (1, 0), (0, 0), (0, 0), (0, 0), (0, 0)))[:, :-1]
        return jnp.concatenate([prev, z], axis=2)

    qb = split(q)
    kc = with_prev(split(k))
    vc = with_prev(split(v))
    sc = jnp.einsum('bnirhd,bnjrhd->bnrhij', qb, kc, preferred_element_type=jnp.float32)
    rel = (jnp.arange(blk)[:, None] + blk) - jnp.arange(2 * blk)[None, :]
    band = (rel >= 0) & (rel <= n_back)
    not_pad = (jnp.arange(nb)[:, None, None] > 0) | (jnp.arange(2 * blk)[None, None, :] >= blk)
    valid = band[None] & not_pad
    bias = bias_j[:, jnp.clip(rel, 0, n_back)]
    sc = jnp.where(valid[None, :, None, None], sc + bias[None, None, None], NEG)
    m = jnp.max(sc, axis=-1, keepdims=True)
    p = jnp.exp(sc - m)
    l = jnp.sum(p, axis=-1)
    o = jnp.einsum('bnrhij,bnjrhd->bnirhd', p, vc)
    l_t = jnp.transpose(l, (0, 1, 4, 2, 3))
    o = o / l_t[..., None]
    lse = jnp.transpose(m[..., 0] + jnp.log(l), (0, 1, 4, 2, 3))
    return o.reshape(b, sp, h, dh)[:, :s], lse.reshape(b, sp, h)[:, :s]


def _dilated_gather_decode(q, k_ext, v_ext, bias_j, dil, n_back, l_buf):
    t = q.shape[1]
    idx = l_buf + jnp.arange(t)[:, None] - dil * jnp.arange(n_back + 1)[None, :]
    valid = idx >= 0
    idx = jnp.maximum(idx, 0)
    kg = k_ext[:, idx]
    vg = v_ext[:, idx]
    sc = jnp.einsum('bthd,btjhd->bhtj', q, kg, preferred_element_type=jnp.float32) + bias_j[None, :, None, :]
    sc = jnp.where(valid[None, None], sc, NEG)
    m = jnp.max(sc, axis=-1, keepdims=True)
    p = jnp.exp(sc - m)
    l = jnp.sum(p, axis=-1)
    o = jnp.einsum('bhtj,btjhd->bthd', p, vg) / jnp.transpose(l, (0, 2, 1))[..., None]
    lse = jnp.transpose(m[..., 0] + jnp.log(l), (0, 2, 1))
    return o, lse


def _mixer_ab(hn, w_in, conv_w, conv_b, w_r, b_r, w_i, b_i, lam, w_out, rel_bias, conv_buf, h0, win_kv):
    b, t, _ = hn.shape
    xa, ga, q, k, v, gb = _split(hn @ w_in, (D_A, D_A, W_B, W_B, W_B, W_B))
    xc, conv_new = _causal_conv(xa, conv_buf, conv_w, conv_b)
    h, h_last = _rglru(xc, h0, w_r, b_r, w_i, b_i, lam)
    ya = h.astype(hn.dtype) * jax.nn.silu(ga)
    q = q.reshape(b, t, H_B, DH_B) * (DH_B ** -0.5)
    k = k.reshape(b, t, H_B, DH_B)
    v = v.reshape(b, t, H_B, DH_B)
    if win_kv is not None:
        l_buf = win_kv[0].shape[1]
        k_ext = jnp.concatenate([win_kv[0].astype(k.dtype), k], axis=1)
        v_ext = jnp.concatenate([win_kv[1].astype(v.dtype), v], axis=1)
    outs, lses = [], []
    for win, dil in DIL_PATTERNS:
        n_back = win // dil
        bias_j = rel_bias[_t5_bucket(dil * jnp.arange(n_back + 1))].T
        if win_kv is None:
            o, lse = _dilated_band_prompt(q, k, v, bias_j, dil, n_back)
        else:
            o, lse = _dilated_gather_decode(q, k_ext, v_ext, bias_j, dil, n_back, l_buf)
        outs.append(o)
        lses.append(lse)
    wts = jax.nn.softmax(jnp.stack(lses), axis=0)
    ob = jnp.einsum('gbth,gbthd->bthd', wts, jnp.stack(outs))
    yb = ob.reshape(b, t, W_B).astype(hn.dtype) * jax.nn.silu(gb)
    y = jnp.concatenate([ya, yb], axis=-1) @ w_out
    if win_kv is None:
        keep = min(WIN_MAX, t)
        k_rows, v_rows = k[:, t - keep:], v[:, t - keep:]
    else:
        k_rows, v_rows = k, v
    return y, conv_new, h_last, k_rows, v_rows


def _diff_core(q, k, v, dist, rel_bias, lam):
    tq, tk = dist.shape
    bias = rel_bias[_t5_bucket(dist)].reshape(tq, tk, 2, H_C)
    bias = jnp.transpose(bias, (2, 3, 0, 1))
    sc = jnp.einsum('bqhcd,bkhcd->bchqk', q, k, preferred_element_type=jnp.float32) + bias
    sc = jnp.where(dist >= 0, sc, NEG)
    p = jax.nn.softmax(sc, axis=-1)
    w = p[:, 0] - lam * p[:, 1]
    return jnp.einsum('bhqk,bkhv->bqhv', w, v)


def _diff_attn_prompt(q, k, v, rel_bias, lam):
    b, s = q.shape[:2]
    nb = s // Q_BLK
    qblocks = jnp.moveaxis(q.reshape(b, nb, Q_BLK, H_C, 2, DH_C), 1, 0)
    kpos = jnp.arange(s)

    def one(args):
        qb, start = args
        qpos = start + jnp.arange(Q_BLK)
        return _diff_core(qb, k, v, qpos[:, None] - kpos[None, :], rel_bias, lam)

    o = lax.map(one, (qblocks, jnp.arange(nb) * Q_BLK))
    return jnp.moveaxis(o, 0, 1).reshape(b, s, H_C, 2 * DH_C)


def _hgrn2(q, f_logit, iv, lb, s0):
    b, t, h, dk = q.shape
    dv = iv.shape[-1]
    f = lb + (1.0 - lb) * jax.nn.sigmoid(f_logit.astype(jnp.float32))
    log_f = jnp.log(f)
    kk = 1.0 - f
    qf = jax.nn.silu(q.astype(jnp.float32)) * (dk ** -0.5)
    vv = iv.astype(jnp.float32)
    c = min(CHUNK_D, t)
    tp = -(-t // c) * c
    n = tp // c
    pad = ((0, 0), (0, tp - t), (0, 0), (0, 0))

    def chunks(z):
        return jnp.pad(z, pad).reshape(b, n, c, h, z.shape[-1])

    qf, kk, vv, log_f = chunks(qf), chunks(kk), chunks(vv), chunks(log_f)
    bc = jnp.cumsum(log_f, axis=2)
    causal = jnp.tril(jnp.ones((c, c), bool))
    diff = bc[:, :, :, None] - bc[:, :, None, :]
    decay = jnp.exp(jnp.where(causal[None, None, :, :, None, None], diff, NEG))
    att = jnp.einsum('bnthd,bnshd,bntshd->bnhts', qf, kk, decay)
    o_intra = jnp.einsum('bnhts,bnshv->bnthv', att, vv)
    b_last = bc[:, :, -1]
    u = jnp.einsum('bnshd,bnshv->bnhdv', kk * jnp.exp(b_last[:, :, None] - bc), vv)
    g = jnp.exp(b_last)

    def step(s, inp):
        g_n, u_n = inp
        return g_n[..., None] * s + u_n, s

    s_fin, s_start = lax.scan(step, s0.astype(jnp.float32), (jnp.moveaxis(g, 1, 0), jnp.moveaxis(u, 1, 0)))
    o_inter = jnp.einsum('bnthd,bnhdv->bnthv', qf * jnp.exp(bc), jnp.moveaxis(s_start, 0, 1))
    o = (o_intra + o_inter).reshape(b, tp, h, dv)[:, :t]
    return o, s_fin


def _mixer_cd(hn, w_in, lam_c, subln, lb, gnorm, w_out, rel_bias, lam_init, s0, kv_past):
    b, t, _ = hn.shape
    qc, kc, vc, gc, qd, fd, idd, gd = _split(hn @ w_in, (W_C, W_C, W_C, W_C, W_DK, W_DK, W_D, W_D))
    qc = qc.reshape(b, t, H_C, 2, DH_C) * (DH_C ** -0.5)
    kc5 = kc.reshape(b, t, H_C, 2, DH_C)
    vc4 = vc.reshape(b, t, H_C, 2 * DH_C)
    lf = lam_c.astype(jnp.float32)
    lam = jnp.exp(jnp.sum(lf[0] * lf[1])) - jnp.exp(jnp.sum(lf[2] * lf[3])) + lam_init
    if kv_past is None:
        oc = _diff_attn_prompt(qc, kc5, vc4, rel_bias, lam)
    else:
        kp, vp = kv_past
        past = kp.shape[1]
        k_all = jnp.concatenate([kp.reshape(b, past, H_C, 2, DH_C).astype(kc5.dtype), kc5], axis=1)
        v_all = jnp.concatenate([vp.astype(vc4.dtype), vc4], axis=1)
        dist = (past + jnp.arange(t))[:, None] - jnp.arange(past + t)[None, :]
        oc = _diff_core(qc, k_all, v_all, dist, rel_bias, lam)
    oc = _rmsnorm(oc, subln) * (1.0 - lam_init)
    yc = oc.reshape(b, t, W_C).astype(hn.dtype) * jax.nn.silu(gc)
    od, s_last = _hgrn2(qd.reshape(b, t, H_D, DK_D), fd.reshape(b, t, H_D, DK_D),
                        idd.reshape(b, t, H_D, DV_D), lb.reshape(H_D, DK_D), s0)
    od = _rmsnorm(od, gnorm)
    yd = od.reshape(b, t, W_D).astype(hn.dtype) * jax.nn.silu(gd)
    y = jnp.concatenate([yc, yd], axis=-1) @ w_out
    return y, kc.reshape(b, t, H_C, 2 * DH_C), vc4, s_last


def setup_inputs(seed: int = 0) -> dict:
    key = jax.random.key(seed)
    ks = iter(jax.random.split(key, 40))

    def nrm(shape, scale):
        return jax.random.normal(next(ks), shape, jnp.float32) * scale

    n_pages = PAST_LEN // PAGE_SIZE
    n_used = DEC_BATCH * n_pages
    n_phys = n_used + max(1, n_used // 4)
    l_buf = min(WIN_MAX, PAST_LEN)
    page_table = jax.random.permutation(next(ks), n_phys)[:n_used].reshape(DEC_BATCH, n_pages).astype(jnp.int32)
    u = jax.random.uniform(next(ks), (N_EVEN, D_A), jnp.float32, 0.9, 0.999)
    a_base = u ** (1.0 / RG_C)
    lam_a = jnp.log(a_base) - jnp.log1p(-a_base)
    bw = D_A // N_BLK_A
    return {
        'x_prompt': nrm((BATCH, SEQ, D_MODEL), 1.0),
        'x_sample': nrm((DEC_BATCH, DEC_SEQ, D_MODEL), 1.0),
        'state_conv_a': nrm((N_EVEN, DEC_BATCH, CONV_W - 1, D_A), 1.0),
        'state_h_a': nrm((N_EVEN, DEC_BATCH, D_A), 0.5),
        'cache_win_k': nrm((N_EVEN, DEC_BATCH, l_buf, H_B, DH_B), 1.0),
        'cache_win_v': nrm((N_EVEN, DEC_BATCH, l_buf, H_B, DH_B), 1.0),
        'cache_k_c': nrm((N_ODD, n_phys, PAGE_SIZE, H_C, 2 * DH_C), 1.0),
        'cache_v_c': nrm((N_ODD, n_phys, PAGE_SIZE, H_C, 2 * DH_C), 1.0),
        'state_s_d': nrm((N_ODD, DEC_BATCH, H_D, DK_D, DV_D), 0.3),
        'page_table': page_table,
        'norm_g': 1.0 + nrm((DEPTH, D_MODEL), 0.05),
        'norm_final': 1.0 + nrm((D_MODEL,), 0.05),
        'rel_bias': nrm((N_BUCKETS, N_BIAS), 0.3),
        'w_in_ab': nrm((N_EVEN, D_MODEL, IN_AB), D_MODEL ** -0.5),
        'conv_w_a': nrm((N_EVEN, CONV_W, D_A), CONV_W ** -0.5),
        'conv_b_a': nrm((N_EVEN, D_A), 0.02),
        'w_r_a': nrm((N_EVEN, N_BLK_A, bw, bw), bw ** -0.5),
        'b_r_a': nrm((N_EVEN, N_BLK_A, bw), 0.02),
        'w_i_a': nrm((N_EVEN, N_BLK_A, bw, bw), bw ** -0.5),
        'b_i_a': nrm((N_EVEN, N_BLK_A, bw), 0.02),
        'lam_a': lam_a,
        'w_out_ab': nrm((N_EVEN, OUT_AB, D_MODEL), OUT_AB ** -0.5),
        'w_in_cd': nrm((N_ODD, D_MODEL, IN_CD), D_MODEL ** -0.5),
        'lam_c': nrm((N_ODD, 4, DH_C), 0.1),
        'subln_c': 1.0 + nrm((N_ODD, 2 * DH_C), 0.05),
        'lb_d': 1.0 + nrm((DEPTH, W_DK), 0.1),
        'gnorm_d': 1.0 + nrm((N_ODD, DV_D), 0.05),
        'w_out_cd': nrm((N_ODD, OUT_CD, D_MODEL), OUT_CD ** -0.5),
    }


def reference(x_prompt, x_sample, state_conv_a, state_h_a, cache_win_k, cache_win_v, cache_k_c, cache_v_c,
              state_s_d, page_table, norm_g, norm_final, rel_bias, w_in_ab, conv_w_a, conv_b_a, w_r_a, b_r_a,
              w_i_a, b_i_a, lam_a, w_out_ab, w_in_cd, lam_c, subln_c, lb_d, gnorm_d, w_out_cd):
    bp = x_prompt.shape[0]
    bs = x_sample.shape[0]
    n_ctx = page_table.shape[1] * PAGE_SIZE
    lb_soft = jax.nn.softmax(lb_d.astype(jnp.float32), axis=0)
    lb_all = jnp.cumsum(lb_soft, axis=0) - lb_soft[0]
    xp, xs = x_prompt, x_sample
    conv_p, h_p, wk_p, wv_p, kc_p, vc_p, s_p = [], [], [], [], [], [], []
    conv_s, h_s, wk_s, wv_s, kc_s, vc_s, s_s = [], [], [], [], [], [], []
    for layer in range(DEPTH):
        hp = _rmsnorm(xp, norm_g[layer])
        hs = _rmsnorm(xs, norm_g[layer])
        if layer % 2 == 0:
            e = layer // 2
            w = (w_in_ab[e], conv_w_a[e], conv_b_a[e], w_r_a[e], b_r_a[e], w_i_a[e], b_i_a[e], lam_a[e],
                 w_out_ab[e], rel_bias)
            yp, c1, h1, k1, v1 = _mixer_ab(hp, *w, jnp.zeros((bp, CONV_W - 1, D_A), hp.dtype),
                                           jnp.zeros((bp, D_A), jnp.float32), None)
            ys, c2, h2, k2, v2 = _mixer_ab(hs, *w, state_conv_a[e], state_h_a[e],
                                           (cache_win_k[e], cache_win_v[e]))
            conv_p.append(c1); h_p.append(h1); wk_p.append(k1); wv_p.append(v1)
            conv_s.append(c2); h_s.append(h2); wk_s.append(k2); wv_s.append(v2)
        else:
            o = layer // 2
            lam_init = 0.8 - 0.6 * math.exp(-0.3 * layer)
            w = (w_in_cd[o], lam_c[o], subln_c[o], lb_all[layer], gnorm_d[o], w_out_cd[o], rel_bias, lam_init)
            yp, k1, v1, s1 = _mixer_cd(hp, *w, jnp.zeros((bp, H_D, DK_D, DV_D), jnp.float32), None)
            kv_past = (cache_k_c[o][page_table].reshape(bs, n_ctx, H_C, 2 * DH_C),
                       cache_v_c[o][page_table].reshape(bs, n_ctx, H_C, 2 * DH_C))
            ys, k2, v2, s2 = _mixer_cd(hs, *w, state_s_d[o], kv_past)
            kc_p.append(k1); vc_p.append(v1); s_p.append(s1)
            kc_s.append(k2); vc_s.append(v2); s_s.append(s2)
        xp = xp + yp
        xs = xs + ys
    y_prompt = _rmsnorm(xp, norm_final)
    y_sample = _rmsnorm(xs, norm_final)
    return (y_prompt, y_sample,
            jnp.stack(conv_p), jnp.stack(h_p), jnp.stack(wk_p), jnp.stack(wv_p),
            jnp.stack(kc_p), jnp.stack(vc_p), jnp.stack(s_p),
            jnp.stack(conv_s), jnp.stack(h_s), jnp.stack(wk_s), jnp.stack(wv_s),
            jnp.stack(kc_s), jnp.stack(vc_s), jnp.stack(s_s))
```

```python
import os
import numpy as np
from contextlib import ExitStack
import concourse.bass as bass
import concourse.mybir as mybir
from concourse.bass_utils import run_bass_kernel_spmd

F32 = mybir.dt.float32
BF16 = mybir.dt.bfloat16
I32 = mybir.dt.int32
ALU = mybir.AluOpType
AF = mybir.ActivationFunctionType
AX = mybir.AxisListType

SEQ = 8192
SEG = 2048
NSS = 64
SEGT = SEG + NSS
T = 4 * SEGT
NSAMP = 64
EPS = 1e-6
LAM_INIT = 0.8 - 0.6 * float(np.exp(-0.3 * 1))
ENGS = ['pe', 'act', 'dve', 'pool', 'sp']


class Prog:
    def __init__(self, nc, es, pfx):
        self.nc, self.es, self.pfx = nc, es, pfx
        self.ins = {e: [] for e in ENGS}
        self.w = {}
        self.r = {}
        self.dcnt = {}
        self.dinc = {}

    def tile(self, name, shape, dt):
        return self.es.enter_context(self.nc.sbuf_tensor(self.pfx + name, shape, dt))

    def psum(self, name, shape, dt):
        return self.es.enter_context(self.nc.psum_tensor(self.pfx + name, shape, dt))

    def _add(self, eng, fn, reads, writes, tag):
        deps = set()
        for k in reads:
            deps |= set(self.w.get(k, {}).values())
        for k in writes:
            deps |= set(self.w.get(k, {}).values())
            deps |= set(self.r.get(k, {}).values())
        if tag is None and eng == 'pe':
            deps = {d for d in deps if not (d[0] == 'c' and d[1] == eng)}
        for d in deps:
            if d[0] == 'c':
                self.ins[d[1]][d[2]]['sig'] = True
        idx = len(self.ins[eng])
        if tag is None:
            ev = ('c', eng, idx)
            ek = eng
        else:
            self.dcnt[tag] = self.dcnt.get(tag, 0) + 1
            ev = ('d', tag, self.dcnt[tag])
            ek = 'd:' + tag
        self.ins[eng].append(dict(fn=fn, deps=deps, sig=False, tag=tag))
        for k in reads:
            self.r.setdefault(k, {})[ek] = ev
        for k in writes:
            self.w[k] = {ek: ev}
            self.r[k] = {}

    def op(self, eng, fn, reads=(), writes=()):
        self._add(eng, fn, list(reads), list(writes), None)

    def dma(self, eng, fn, reads, writes, tag, inc=16):
        self.dinc[tag] = inc
        self._add(eng, fn, list(reads), list(writes), tag)

    def emit(self):
        nc, es = self.nc, self.es
        sems = {e: nc.alloc_semaphore(name=self.pfx + 's_' + e) for e in ENGS}
        dsems = {t: nc.alloc_semaphore(name=self.pfx + 'd_' + t) for t in self.dcnt}
        sig = {}
        for e in ENGS:
            c = 0
            for i, it in enumerate(self.ins[e]):
                if it['sig']:
                    c += 1
                    sig[(e, i)] = c
        ins = self.ins
        dcnt = self.dcnt

        def run(e, h):
            waited = {}
            for i, it in enumerate(ins[e]):
                for d in sorted(it['deps'], key=str):
                    if d[0] == 'c':
                        key, val, sem = 'c' + d[1], sig[(d[1], d[2])], sems[d[1]]
                    else:
                        key, val, sem = 'd' + d[1], self.dinc[d[1]] * d[2], dsems[d[1]]
                    if waited.get(key, 0) >= val:
                        continue
                    waited[key] = val
                    h.wait_ge(sem, val)
                r = it['fn'](h)
                if it['tag'] is not None:
                    r.then_inc(dsems[it['tag']], self.dinc[it['tag']])
                elif it['sig']:
                    r.then_inc(sems[e], 1)
            if e == 'sp':
                for t, c in dcnt.items():
                    if waited.get('d' + t, 0) < self.dinc[t] * c:
                        h.wait_ge(dsems[t], self.dinc[t] * c)
                for e2 in ENGS:
                    if e2 == 'sp':
                        continue
                    c = max([v for (ee, _), v in sig.items() if ee == e2], default=0)
                    if c and waited.get('c' + e2, 0) < c:
                        h.wait_ge(sems[e2], c)

        with nc.Block() as block:
            @block.tensor
            def _(h):
                run('pe', h)

            @block.scalar
            def _(h):
                run('act', h)

            @block.vector
            def _(h):
                run('dve', h)

            @block.gpsimd
            def _(h):
                run('pool', h)

            @block.sync
            def _(h):
                run('sp', h)

        nc.all_engine_barrier()
        nc.clear_and_free_semaphores(list(sems.values()) + list(dsems.values()))
        nc.all_engine_barrier()


def token_tiles():
    out = []
    for r in range(4):
        for i in range(4):
            out.append((r, 'p', r * SEGT + 512 * i, 512, r * SEG + 512 * i))
        out.append((r, 's', r * SEGT + SEG, NSS, r * NSS))
    return out


def rmsnorm_transpose(P, x_dram, row0, ntok, xt, ss, rstd, xn, junk, g_rep, ident, tp, hnT, xtag, li):
    nb = max(1, ntok // 128)
    pp = min(ntok, 128)
    for hb in range(0, nb, 2):
        nbb = min(2, nb - hb)
        rows = x_dram[row0 + hb * 128: row0 + hb * 128 + nbb * pp, :].rearrange("(b p) d -> p b d", p=pp)
        P.dma('sp', lambda h, rows=rows, nbb=nbb: h.dma_start(out=xt[0:pp, 0:nbb, :], in_=rows),
              [], ['xt'], xtag)
        for bb in range(nbb):
            blk = hb + bb
            P.op('act', lambda h, bb=bb, blk=blk: h.activation(out=junk[0:pp, :], in_=xt[0:pp, bb, :], func=AF.Square,
                                                               accum_out=ss[0:pp, blk:blk + 1]),
                 ['xt'], ['junk', 'ss'])
            P.op('dve', lambda h, blk=blk: h.tensor_scalar(rstd[0:pp, blk:blk + 1], ss[0:pp, blk:blk + 1],
                                                           1.0 / 1024, EPS, ALU.mult, ALU.add),
                 ['ss'], ['rstd'])
            P.op('act', lambda h, blk=blk: h.activation(out=rstd[0:pp, blk:blk + 1], in_=rstd[0:pp, blk:blk + 1],
                                                        func=AF.Sqrt), ['rstd'], ['rstd'])
            P.op('dve', lambda h, blk=blk: h.reciprocal(rstd[0:pp, blk:blk + 1], rstd[0:pp, blk:blk + 1]),
                 ['rstd'], ['rstd'])
            P.op('dve', lambda h, bb=bb, blk=blk: h.scalar_tensor_tensor(
                xn[0:pp, :], xt[0:pp, bb, :], rstd[0:pp, blk:blk + 1], g_rep[0:pp, li, :], ALU.mult, ALU.mult),
                 ['xt', 'rstd', 'g_rep'], ['xn'])
            for c in range(8):
                P.op('pe', lambda h, c=c: h.transpose(tp[:, c, 0:pp], xn[0:pp, c * 128:(c + 1) * 128],
                                                      ident[0:pp, 0:pp]),
                     ['xn', 'ident'], ['tp'])
            eng = 'act' if blk % 2 == 0 else 'dve'
            if eng == 'act':
                P.op('act', lambda h, blk=blk: h.copy(hnT[:, :, blk * 128: blk * 128 + pp], tp[:, :, 0:pp]),
                     ['tp'], ['hnT'])
            else:
                P.op('dve', lambda h, blk=blk: h.tensor_copy(hnT[:, :, blk * 128: blk * 128 + pp], tp[:, :, 0:pp]),
                     ['tp'], ['hnT'])


def all_gather_mix(P, D, L):
    for r_ in range(4):
        for hf in range(2):
            nm = 'mp%d_%d_%d' % (L, hf, r_)
            gn = 'mg%d_%d_%d' % (L, hf, r_)
            P.dma('pool', lambda h, nm=nm, gn=gn: h.collective_compute("AllGather", ALU.bypass, replica_groups=[[0, 1, 2, 3], [4, 5, 6, 7]],
                                                                 ins=[D[nm]], outs=[D[gn]]),
                  [nm], [gn], 'cc', inc=1)


def out_proj(P, D, L, wout_name, xin_name, xout_name, final, ng_li, xbuf, xbuf_name):
    Wo = P.tile('Wo', [128, 8, 1024], BF16)
    mixT = [P.tile('mixT%d' % i, [128, 8, 128], BF16) for i in range(2)]
    xin = [xbuf[:, i * 1024:(i + 1) * 1024] for i in range(2)]
    oss = P.tile('oss', [128, 2], F32)
    P.dma('pool', lambda h: h.dma_start(out=Wo[:], in_=D[wout_name].rearrange("(c p) f -> p c f", p=128)), [], ['Wo'], 'Wo')
    if final:
        gf = P.tile('gf', [128, 1, 1024], F32)
        ojunk = P.tile('ojunk', [128, 1024], BF16)
        P.dma('sp', lambda h: h.dma_start(out=gf[:], in_=D['ng'][ng_li:ng_li + 1, :].partition_broadcast(128)), [], ['gf'], 'gf')
    nseg = int(os.environ.get('K_NOSEG', '4'))
    cnt = 0
    for r_ in range(nseg):
        for b_ in range(17):
            n = 128 if b_ < 16 else 64
            c = b_ * 128
            c0 = r_ * SEGT + c
            i = cnt % 2
            cnt += 1
            xb = xbuf_name + '%d' % i
            for hf in range(2):
                gn = 'mg%d_%d_%d' % (L, hf, r_)
                P.dma('sp', lambda h, i=i, c=c, n=n, hf=hf, gn=gn: h.dma_start(
                    out=mixT[i][:, hf * 4:(hf + 1) * 4, 0:n], in_=D[gn][:, c:c + n].rearrange("(g p) t -> p g t", p=128)),
                    [gn], ['mixT%d' % i], 'mixT%d' % i)
            P.dma('sp', lambda h, i=i, c0=c0, n=n: h.dma_start(out=xin[i][0:n, :], in_=D[xin_name][c0:c0 + n, :]), [xin_name], [xb], 'xin%d' % i)
            for half in range(2):
                A = P.acc[half]
                for cc_ in range(8):
                    P.op('pe', lambda h, A=A, i=i, cc_=cc_, half=half, n=n: h.matmul(A[0:n, 0:512], mixT[i][:, cc_, 0:n], Wo[:, cc_, half * 512:(half + 1) * 512],
                                                                                 start=(cc_ == 0), stop=(cc_ == 7)),
                         ['mixT%d' % i, 'Wo'], ['acc%d' % half])
                P.op('dve', lambda h, A=A, i=i, half=half, n=n: h.tensor_tensor(xin[i][0:n, half * 512:(half + 1) * 512], xin[i][0:n, half * 512:(half + 1) * 512],
                                                                              A[0:n, 0:512], ALU.add),
                     ['acc%d' % half, xb], [xb])
            if final:
                P.op('act', lambda h, i=i, n=n: h.activation(out=ojunk[0:n, :], in_=xin[i][0:n, :], func=AF.Square, accum_out=oss[0:n, 0:1]), [xb], ['ojunk', 'oss'])
                P.op('dve', lambda h, n=n: h.tensor_scalar(oss[0:n, 1:2], oss[0:n, 0:1], 1.0 / 1024, EPS, ALU.mult, ALU.add), ['oss'], ['oss'])
                P.op('act', lambda h, n=n: h.activation(out=oss[0:n, 1:2], in_=oss[0:n, 1:2], func=AF.Sqrt), ['oss'], ['oss'])
                P.op('dve', lambda h, n=n: h.reciprocal(oss[0:n, 1:2], oss[0:n, 1:2]), ['oss'], ['oss'])
                P.op('dve', lambda h, i=i, n=n: h.scalar_tensor_tensor(xin[i][0:n, :], xin[i][0:n, :], oss[0:n, 1:2], gf[0:n, 0, :], ALU.mult, ALU.mult),
                     [xb, 'oss', 'gf'], [xb])
            P.dma('pool', lambda h, i=i, c0=c0, n=n: h.dma_start(out=D[xout_name][c0:c0 + n, :], in_=xin[i][0:n, :]), [xb], [xout_name], 'xo%d' % i)


def build_layer0(nc, D, stage):
    with ExitStack() as es:
        P = Prog(nc, es, 'l0_')
        W = P.tile('W', [128, 8, 768], BF16)
        g_rep = P.tile('g_rep', [128, 1, 1024], F32)
        ident = P.tile('ident', [128, 128], BF16)
        vecA = P.tile('vecA', [128, 8], F32)
        sc_a = P.tile('sc_a', [128, 1], F32)
        wr = P.tile('wr', [128, 128], BF16)
        wi = P.tile('wi', [128, 128], BF16)
        qT = P.tile('qT', [128, SEQ], BF16)
        kT = P.tile('kT', [128, SEQ], BF16)
        vT = P.tile('vT', [128, SEQ], BF16)
        gbT = P.tile('gbT', [128, SEQ], BF16)
        qTs = P.tile('qTs', [128, 256], BF16)
        kTs = P.tile('kTs', [128, 256], BF16)
        vTs = P.tile('vTs', [128, 256], BF16)
        gbTs = P.tile('gbTs', [128, 256], BF16)
        xt = P.tile('xt', [128, 2, 1024], F32)
        xn = P.tile('xn', [128, 1024], BF16)
        junk = P.tile('junk', [128, 1024], BF16)
        ss = P.tile('ss', [128, 4], F32)
        rstd = P.tile('rstd', [128, 4], F32)
        hnT = P.tile('hnT', [128, 8, 512], BF16)
        xa = P.tile('xa', [128, 3 + 512], F32)
        sga = P.tile('sga', [128, 512], F32)
        t0 = P.tile('t0', [128, 512], F32)
        t1 = P.tile('t1', [128, 512], F32)
        t2 = P.tile('t2', [128, 512], F32)
        t3 = P.tile('t3', [128, 512], F32)
        xcb = P.tile('xcb', [128, 512], BF16)
        ya = P.tile('ya', [128, 512], BF16)
        hc = P.tile('hc', [128, 1], F32)
        kst = P.tile('kst', [128, 512], F32)
        vst = P.tile('vst', [128, 512], F32)
        xs7 = P.tile('xs7', [128, 16, 7], F32)
        a5 = P.tile('a5', [128, 16, 5], F32)
        u5 = P.tile('u5', [128, 16, 5], F32)
        h5 = P.tile('h5', [128, 16, 5], F32)
        h0s = P.tile('h0s', [128, 64], F32)
        hls = P.tile('hls', [128, 64], F32)
        cvs = P.tile('cvs', [128, 64, 3], F32)
        cvo = P.tile('cvo', [128, 64, 3], F32)
        EB = P.tile('EB', [128, 3, 2, 256], F32)
        maskB = P.tile('maskB', [128, 256], F32)
        EBs = P.tile('EBs', [128, 8, 2, 4], F32)
        multBs = P.tile('multBs', [128, 8, 2, 4], F32)
        ones = P.tile('ones', [128, 64], BF16)
        NUM = P.tile('NUM', [128, 2048], F32)
        DEN = P.tile('DEN', [128, 2048], F32)
        yb = P.tile('yb', [128, 2048], BF16)
        pT = [P.tile('pT%d' % i, [128, 2, 128], BF16) for i in range(2)]
        pF = [P.tile('pF%d' % i, [128, 2, 128], F32) for i in range(2)]
        vtok = [P.tile('vtok%d' % i, [128, 128], BF16) for i in range(2)]
        kx = [P.tile('kx%d' % i, [128, 1024], BF16) for i in range(2)]
        vx = [P.tile('vx%d' % i, [128, 8, 128], BF16) for i in range(2)]
        vnew = P.tile('vnew', [4, 128], BF16)
        pTs = P.tile('pTs', [128, 8, 2, 4], BF16)
        pFs = P.tile('pFs', [128, 8, 2, 4], F32)
        tp = P.psum('tp', [128, 8, 128], BF16)
        acc = [P.psum('acc%d' % i, [128, 512], F32) for i in range(3)]
        gps = [P.psum('gps%d' % i, [128, 512], F32) for i in range(2)]

        P.dma('pool', lambda h: h.dma_start(out=W[:], in_=D['w_in0'].rearrange("(c p) f -> p c f", p=128)),
              [], ['W'], 'W')
        P.dma('pool', lambda h: h.dma_start(out=ident[:], in_=D['ident']), [], ['ident'], 'ident')
        P.dma('pool', lambda h: h.dma_start(out=wr[:], in_=D['wr']), [], ['wr'], 'wr')
        P.dma('pool', lambda h: h.dma_start(out=wi[:], in_=D['wi']), [], ['wi'], 'wi')
        P.dma('sp', lambda h: h.dma_start(out=g_rep[:, 0:1, :], in_=D['ng'][0:1, :].partition_broadcast(128)),
              [], ['g_rep'], 'g_rep')
        P.dma('sp', lambda h: h.dma_start(out=vecA[:], in_=D['vecA']), [], ['vecA'], 'vecA')
        P.dma('sp', lambda h: h.dma_start(out=h0s[:], in_=D['h0s']), [], ['h0s'], 'h0s')
        P.dma('sp', lambda h: h.dma_start(out=cvs[:], in_=D['convs']), [], ['cvs'], 'cvs')
        P.dma('sp', lambda h: h.dma_start(out=EB[:], in_=D['biasB']), [], ['EB'], 'EB')
        P.dma('sp', lambda h: h.dma_start(out=maskB[:], in_=D['maskB']), [], ['maskB'], 'maskB')
        P.dma('sp', lambda h: h.dma_start(out=EBs[:], in_=D['biasBs']), [], ['EBs'], 'EBs')
        P.dma('sp', lambda h: h.dma_start(out=multBs[:], in_=D['multBs']), [], ['multBs'], 'multBs')
        P.op('act', lambda h: h.activation(out=EB[:], in_=EB[:], func=AF.Exp), ['EB'], ['EB'])
        for pat in range(3):
            for hh_ in range(2):
                P.op('dve', lambda h, pat=pat, hh_=hh_: h.tensor_tensor(EB[:, pat, hh_, :], EB[:, pat, hh_, :], maskB[:], ALU.mult),
                     ['EB', 'maskB'], ['EB'])
        P.op('act', lambda h: h.activation(out=EBs[:], in_=EBs[:], func=AF.Exp), ['EBs'], ['EBs'])
        P.op('dve', lambda h: h.tensor_tensor(EBs[:], EBs[:], multBs[:], ALU.mult), ['EBs', 'multBs'], ['EBs'])
        P.op('dve', lambda h: h.memset(ones[:], 1.0), [], ['ones'])
        P.op('act', lambda h: h.activation(out=sc_a[:], in_=vecA[:, 7:8], func=AF.Exp, scale=-1.0), ['vecA'], ['sc_a'])
        P.op('act', lambda h: h.activation(out=sc_a[:], in_=sc_a[:], func=AF.Ln, bias=1.0), ['sc_a'], ['sc_a'])
        P.op('dve', lambda h: h.tensor_scalar(sc_a[:], sc_a[:], -8.0, None, ALU.mult), ['sc_a'], ['sc_a'])
        P.op('dve', lambda h: h.memset(xa[:, 0:3], 0.0), [], ['xa'])
        P.op('dve', lambda h: h.memset(hc[:], 0.0), [], ['hc'])
        P.op('dve', lambda h: h.memset(a5[:], 0.0), [], ['a5'])

        cw = [vecA[:, j:j + 1] for j in range(4)]
        cb, b_r, b_i = vecA[:, 4:5], vecA[:, 5:6], vecA[:, 6:7]

        def gates_and_u(n, xc, xcb_, r_t, gi_t):
            P.op('act', lambda h: h.copy(xcb_, xc), ['xc'], ['xcb'])
            P.op('pe', lambda h: h.matmul(gps[0][:, 0:n], wr[:], xcb_, start=True, stop=True), ['xcb', 'wr'], ['gps0'])
            P.op('pe', lambda h: h.matmul(gps[1][:, 0:n], wi[:], xcb_, start=True, stop=True), ['xcb', 'wi'], ['gps1'])
            P.op('act', lambda h: h.activation(out=r_t, in_=gps[0][:, 0:n], func=AF.Sigmoid, bias=b_r),
                 ['gps0', 'vecA'], ['r_t'])
            P.op('act', lambda h: h.activation(out=gi_t, in_=gps[1][:, 0:n], func=AF.Sigmoid, bias=b_i),
                 ['gps1', 'vecA'], ['gi_t'])
            P.op('act', lambda h: h.activation(out=r_t, in_=r_t, func=AF.Exp, scale=sc_a[:]), ['r_t', 'sc_a'], ['r_t'])
            P.op('dve', lambda h: h.tensor_tensor(gi_t, gi_t, xc, ALU.mult), ['gi_t', 'xc'], ['gi_t'])

        def do_tile(r, kind, col0, ntok, lc0):
            nmm = max(ntok, 128)
            rmsnorm_transpose(P, D['x'], col0, nmm, xt, ss, rstd, xn, junk, g_rep, ident, tp, hnT, 'xt', 0)
            n = ntok
            for fc in range(6):
                A = acc[fc % 3]
                an = 'acc%d' % (fc % 3)
                for kc in range(8):
                    P.op('pe', lambda h, A=A, kc=kc, fc=fc: h.matmul(A[:, 0:nmm], W[:, kc, fc * 128:(fc + 1) * 128],
                                                                    hnT[:, kc, 0:nmm], start=(kc == 0), stop=(kc == 7)),
                         ['W', 'hnT'], [an])
                if kind == 'p':
                    dq, dk, dv, dg = qT[:, lc0:lc0 + n], kT[:, lc0:lc0 + n], vT[:, lc0:lc0 + n], gbT[:, lc0:lc0 + n]
                    names = ('qT', 'kT', 'vT', 'gbT')
                else:
                    dq, dk, dv, dg = qTs[:, lc0:lc0 + n], kTs[:, lc0:lc0 + n], vTs[:, lc0:lc0 + n], gbTs[:, lc0:lc0 + n]
                    names = ('qTs', 'kTs', 'vTs', 'gbTs')
                if fc == 0:
                    if kind == 'p':
                        P.op('act', lambda h, A=A: h.copy(xa[:, 3:3 + n], A[:, 0:n]), [an], ['xa'])
                    elif not os.environ.get('K_NOXS7'):
                        P.op('act', lambda h, A=A: h.copy(xs7[:, :, 3:7], A[:, 0:n].rearrange("p (s t) -> p s t", t=4)),
                             [an], ['xs7'])
                elif fc == 1:
                    P.op('act', lambda h, A=A: h.activation(out=sga[:, 0:n], in_=A[:, 0:n], func=AF.Silu), [an], ['sga'])
                elif fc == 2:
                    P.op('act', lambda h, A=A, dq=dq: h.activation(out=dq, in_=A[:, 0:n], func=AF.Copy, scale=0.125),
                         [an], [names[0]])
                elif fc in (3, 4):
                    dst = dk if fc == 3 else dv
                    st = kst if fc == 3 else vst
                    stn = 'kst' if fc == 3 else 'vst'
                    outk = 'wk' if fc == 3 else 'wv'
                    need_out = (kind == 'p' and r == 3) or kind == 's'
                    if not need_out:
                        P.op('dve', lambda h, A=A, dst=dst: h.tensor_copy(dst, A[:, 0:n]), [an], [names[fc - 2]])
                    else:
                        P.op('dve', lambda h, A=A, st=st: h.tensor_copy(st[:, 0:n], A[:, 0:n]), [an], [stn])
                        P.op('act', lambda h, st=st, dst=dst: h.copy(dst, st[:, 0:n]), [stn], [names[fc - 2]])
                        if kind == 'p':
                            oc = lc0 - 3 * SEG
                            P.dma('sp', lambda h, st=st, outk=outk, oc=oc: h.dma_start(out=D[outk + '_p'][oc // 512], in_=st[:, 0:n]),
                                  [stn], [], stn)
                        else:
                            P.dma('sp', lambda h, st=st, outk=outk: h.dma_start(out=D[outk + '_s'][r], in_=st[:, 0:n]),
                                  [stn], [], stn)
                else:
                    P.op('act', lambda h, A=A, dg=dg: h.activation(out=dg, in_=A[:, 0:n], func=AF.Silu), [an], [names[3]])
            if os.environ.get('K_NOA'):
                return
            if kind == 'p':
                xc = t0[:, 0:n]
                P.op('dve', lambda h: h.tensor_scalar(xc, xa[:, 3:3 + n], cw[3], cb, ALU.mult, ALU.add), ['xa', 'vecA'], ['xc'])
                for j in (1, 2, 3):
                    P.op('dve', lambda h, j=j: h.scalar_tensor_tensor(xc, xa[:, 3 - j:3 - j + n], cw[3 - j], xc, ALU.mult, ALU.add),
                         ['xa', 'xc', 'vecA'], ['xc'])
                P.op('dve', lambda h: h.tensor_copy(xa[:, 0:3], xa[:, n:n + 3]), ['xa'], ['xa'])
                r_t, gi_t = t1[:, 0:n], t2[:, 0:n]
                gates_and_u(n, xc, xcb[:, 0:n], r_t, gi_t)
                a2 = t3[:, 0:n]
                P.op('dve', lambda h: h.tensor_tensor(a2, r_t, r_t, ALU.mult), ['r_t'], ['a2'])
                P.op('dve', lambda h: h.tensor_scalar(a2, a2, -1.0, 1.0, ALU.mult, ALU.add), ['a2'], ['a2'])
                P.op('act', lambda h: h.activation(out=a2, in_=a2, func=AF.Sqrt), ['a2'], ['a2'])
                P.op('dve', lambda h: h.tensor_tensor(gi_t, gi_t, a2, ALU.mult), ['gi_t', 'a2'], ['gi_t'])
                hh = t3[:, 0:n]
                P.op('dve', lambda h: h.tensor_tensor_scan(hh, r_t, gi_t, hc[:], ALU.mult, ALU.add),
                     ['r_t', 'gi_t', 'hc', 'a2'], ['a2'])
                P.op('dve', lambda h: h.tensor_copy(hc[:], hh[:, n - 1:n]), ['a2'], ['hc'])
                P.op('dve', lambda h: h.tensor_tensor(ya[:, 0:n], hh, sga[:, 0:n], ALU.mult), ['a2', 'sga'], ['ya'])
                P.dma('pool', lambda h: h.dma_start(out=D['mp0_0_%d' % r][:, col0 - r * SEGT:col0 - r * SEGT + n], in_=ya[:, 0:n]),
                      ['ya'], ['mp0_0_%d' % r], 'ya')
                if r == 3 and lc0 + n == SEQ:
                    P.dma('pool', lambda h: h.dma_start(out=D['conv_p'], in_=xa[:, 0:3]), ['xa'], [], 'xa_o')
                    P.dma('pool', lambda h: h.dma_start(out=D['h_p'], in_=hc[:]), ['hc'], [], 'hc_o')
            else:
                sstop = int(os.environ.get('K_SSTOP', '99'))
                if sstop <= 2:
                    return
                P.op('dve', lambda h: h.tensor_copy(xs7[:, :, 0:3], cvs[:, 16 * r:16 * r + 16, :]), ['cvs'], ['xs7'])
                xc3 = t0[:, 0:64].rearrange("p (s t) -> p s t", t=4)
                xc = t0[:, 0:64]
                P.op('dve', lambda h: h.tensor_scalar(xc3, xs7[:, :, 3:7], cw[3], cb, ALU.mult, ALU.add), ['xs7', 'vecA'], ['xc'])
                for j in (1, 2, 3):
                    P.op('dve', lambda h, j=j: h.scalar_tensor_tensor(xc3, xs7[:, :, 3 - j:7 - j], cw[3 - j], xc3, ALU.mult, ALU.add),
                         ['xs7', 'xc', 'vecA'], ['xc'])
                P.op('dve', lambda h: h.tensor_copy(cvo[:, 16 * r:16 * r + 16, :], xs7[:, :, 4:7]), ['xs7'], ['cvo'])
                if r == 3:
                    P.dma('pool', lambda h: h.dma_start(out=D['conv_s'], in_=cvo[:]), ['cvo'], [], 'cvo_o')
                if sstop <= 3:
                    return
                r_t, gi_t = t1[:, 0:64], t2[:, 0:64]
                gates_and_u(64, xc, xcb[:, 0:64], r_t, gi_t)
                if sstop <= 4:
                    return
                a2 = t3[:, 0:64]
                P.op('dve', lambda h: h.tensor_tensor(a2, r_t, r_t, ALU.mult), ['r_t'], ['a2'])
                P.op('dve', lambda h: h.tensor_scalar(a2, a2, -1.0, 1.0, ALU.mult, ALU.add), ['a2'], ['a2'])
                P.op('act', lambda h: h.activation(out=a2, in_=a2, func=AF.Sqrt), ['a2'], ['a2'])
                P.op('dve', lambda h: h.tensor_tensor(gi_t, gi_t, a2, ALU.mult), ['gi_t', 'a2'], ['gi_t'])
                P.op('dve', lambda h: h.tensor_copy(a5[:, :, 1:5], r_t.rearrange("p (s t) -> p s t", t=4)), ['r_t'], ['a5'])
                P.op('dve', lambda h: h.tensor_copy(u5[:, :, 1:5], gi_t.rearrange("p (s t) -> p s t", t=4)), ['gi_t'], ['u5'])
                P.op('dve', lambda h, r=r: h.tensor_copy(u5[:, :, 0:1], h0s[:, 16 * r:16 * r + 16].rearrange("p (s o) -> p s o", o=1)),
                     ['h0s'], ['u5'])
                if sstop <= 5:
                    return
                P.op('dve', lambda h: h.tensor_tensor_scan(h5[:].rearrange("p s t -> p (s t)"), a5[:].rearrange("p s t -> p (s t)"),
                                                           u5[:].rearrange("p s t -> p (s t)"), 0.0, ALU.mult, ALU.add),
                     ['a5', 'u5'], ['h5'])
                P.op('dve', lambda h: h.tensor_tensor(ya[:, 0:64].rearrange("p (s t) -> p s t", t=4), h5[:, :, 1:5],
                                                      sga[:, 0:64].rearrange("p (s t) -> p s t", t=4), ALU.mult),
                     ['h5', 'sga'], ['ya'])
                P.dma('pool', lambda h: h.dma_start(out=D['mp0_0_%d' % r][:, SEG:SEGT], in_=ya[:, 0:64]),
                      ['ya'], ['mp0_0_%d' % r], 'ya')
                P.op('dve', lambda h: h.tensor_copy(hls[:, 16 * r:16 * r + 16].rearrange("p (s o) -> p s o", o=1), h5[:, :, 4:5]),
                     ['h5'], ['hls'])
                if r == 3:
                    P.dma('pool', lambda h: h.dma_start(out=D['h_s'], in_=hls[:]), ['hls'], [], 'h5_o')
        lim = int(os.environ.get('K_NT', '99'))
        for tt in token_tiles()[:lim]:
            do_tile(*tt)
        if stage >= 2:
            pass

            DILS = (1, 4, 16)
            spsh, nps, dps = (acc[0], gps[0]), acc[1], acc[2]
            spsn = ('acc0', 'gps0')

            def mixb_qblock(sb, pat, res, nq, first_pat):
                dil = DILS[pat]
                N = sb * (16 // dil) + nq
                qs = (N * 128) * dil + res
                qsl = slice(qs, qs + 127 * dil + 1, dil)
                kbs = [Nk for Nk in (N - 1, N) if Nk >= 0]
                for ki, Nk in enumerate(kbs):
                    ks = (Nk * 128) * dil + res
                    ksl = slice(ks, ks + 127 * dil + 1, dil)
                    i = ki
                    for h_ in range(2):
                        P.op('pe', lambda h, h_=h_, ksl=ksl: h.matmul(spsh[h_][:, 0:128], kT[64 * h_:64 * h_ + 64, ksl],
                                                                      qT[64 * h_:64 * h_ + 64, qsl], start=True, stop=True),
                             ['kT', 'qT'], [spsn[h_]])
                    for h_ in range(2):
                        P.op('act', lambda h, i=i, h_=h_: h.activation(out=pF[i][:, h_, :], in_=spsh[h_][:, 0:128], func=AF.Exp),
                             [spsn[h_]], ['pF%d' % i])
                    off = 0 if Nk == N else 128
                    P.op('dve', lambda h, i=i, off=off: h.tensor_tensor(pT[i][:], pF[i][:], EB[:, pat, :, off:off + 128], ALU.mult),
                         ['pF%d' % i, 'EB'], ['pT%d' % i])
                    P.op('pe', lambda h, ksl=ksl: h.transpose(tp[:, 0, :], vT[:, ksl], ident[:]), ['vT', 'ident'], ['tp'])
                    P.op('act', lambda h, i=i: h.copy(vtok[i][:], tp[:, 0, :]), ['tp'], ['vtok%d' % i])
                for h_ in range(2):
                    for ki in range(len(kbs)):
                        P.op('pe', lambda h, h_=h_, ki=ki: h.matmul(nps[64 * h_:64 * h_ + 64, 0:128], vtok[ki][:, 64 * h_:64 * h_ + 64],
                                                                    pT[ki][:, h_, :], start=(ki == 0), stop=(ki == len(kbs) - 1)),
                             ['vtok%d' % ki, 'pT%d' % ki], ['acc1'])
                    for ki in range(len(kbs)):
                        P.op('pe', lambda h, h_=h_, ki=ki: h.matmul(dps[64 * h_:64 * h_ + 64, 0:128], ones[:, 0:64],
                                                                    pT[ki][:, h_, :], start=(ki == 0), stop=(ki == len(kbs) - 1)),
                             ['ones', 'pT%d' % ki], ['acc2'])
                ls = (nq * 128) * dil + res
                lsl = slice(ls, ls + 127 * dil + 1, dil)
                if first_pat:
                    P.op('dve', lambda h: h.tensor_copy(NUM[:, lsl], nps[:, 0:128]), ['acc1'], ['NUM'])
                    P.op('act', lambda h: h.copy(DEN[:, lsl], dps[:, 0:128]), ['acc2'], ['DEN'])
                else:
                    P.op('dve', lambda h: h.tensor_tensor(NUM[:, lsl], NUM[:, lsl], nps[:, 0:128], ALU.add), ['acc1', 'NUM'], ['NUM'])
                    P.op('dve', lambda h: h.tensor_tensor(DEN[:, lsl], DEN[:, lsl], dps[:, 0:128], ALU.add), ['acc2', 'DEN'], ['DEN'])

            def mixb_sb(sb):
                for pat in range(3):
                    dil = DILS[pat]
                    for res in range(dil):
                        for nq in range(16 // dil):
                            mixb_qblock(sb, pat, res, nq, pat == 0)
                P.op('dve', lambda h: h.reciprocal(DEN[:], DEN[:]), ['DEN'], ['DEN'])
                P.op('dve', lambda h: h.tensor_tensor(NUM[:], NUM[:], DEN[:], ALU.mult), ['NUM', 'DEN'], ['NUM'])
                P.op('dve', lambda h: h.tensor_tensor(yb[:], NUM[:], gbT[:, sb * SEG:(sb + 1) * SEG], ALU.mult), ['NUM', 'gbT'], ['yb'])
                P.dma('pool', lambda h: h.dma_start(out=D['mp0_1_%d' % sb][:, 0:SEG], in_=yb[:]),
                      ['yb'], ['mp0_1_%d' % sb], 'yb')

            nsb = int(os.environ.get('K_NSB', '4'))
            for sb in range(nsb):
                mixb_sb(sb)

            spssh = (acc[0], gps[0])
            npss, dpss = gps[1], acc[1]
            for i_ in range(2):
                P.op('dve', lambda h, i_=i_: h.memset(kx[i_][:, 896:1024], 0.0), [], ['kx%d' % i_])
                P.op('dve', lambda h, i_=i_: h.memset(vx[i_][:, 7, :], 0.0), [], ['vx%d' % i_])

            def mixb_sample(s_):
                i = s_ % 2
                c4 = slice(4 * s_, 4 * s_ + 4)
                P.dma('pool', lambda h: h.dma_start(out=kx[i][:, 0:896], in_=D['kwin'][s_]), [], ['kx%d' % i], 'kx%d' % i)
                P.dma('pool', lambda h: h.dma_start(out=vx[i][:, 0:7, :], in_=D['vwin'][s_].rearrange("(b p) f -> p b f", p=128)),
                      [], ['vx%d' % i], 'vx%d' % i)
                P.op('dve', lambda h: h.tensor_copy(kx[i][:, 896:900], kTs[:, c4]), ['kTs'], ['kx%d' % i])
                P.op('pe', lambda h: h.transpose(tp[0:4, 1, :], vTs[:, c4], ident[:]), ['vTs', 'ident'], ['tp'])
                P.op('act', lambda h: h.copy(vx[i][0:4, 7, :], tp[0:4, 1, :]), ['tp'], ['vx%d' % i])
                svh = [spssh[h_][:, 0:32].rearrange("p (b t) -> p b t", b=8) for h_ in range(2)]
                for h_ in range(2):
                    for blk in range(8):
                        P.op('pe', lambda h, blk=blk, h_=h_: h.matmul(svh[h_][:, blk, :], kx[i][64 * h_:64 * h_ + 64, blk * 128:(blk + 1) * 128],
                                                                     qTs[64 * h_:64 * h_ + 64, c4], start=True, stop=True),
                             ['kx%d' % i, 'qTs'], [spsn[h_]])
                for h_ in range(2):
                    P.op('act', lambda h, h_=h_: h.activation(out=pFs[:, :, h_, :], in_=svh[h_], func=AF.Exp), [spsn[h_]], ['pFs'])
                P.op('dve', lambda h: h.tensor_tensor(pTs[:], pFs[:], EBs[:], ALU.mult), ['pFs', 'EBs'], ['pTs'])
                for h_ in range(2):
                    for (dst, dn, isnum) in ((npss, 'gps1', True), (dpss, 'acc1', False)):
                        for blk in range(8):
                            lh = vx[i][:, blk, 64 * h_:64 * h_ + 64] if isnum else ones[:, 0:64]
                            rh = pTs[:, blk, h_, :]
                            P.op('pe', lambda h, dst=dst, lh=lh, rh=rh, blk=blk, h_=h_: h.matmul(dst[64 * h_:64 * h_ + 64, c4], lh, rh,
                                                                                               start=(blk == 0), stop=(blk == 7)),
                                 ['vx%d' % i, 'ones', 'pTs'], [dn])

            nsm = int(os.environ.get('K_NSM', '64'))
            for s_ in range(nsm):
                mixb_sample(s_)
            P.op('dve', lambda h: h.reciprocal(DEN[:, 0:256], dpss[:, 0:256]), ['acc1'], ['DEN'])
            P.op('dve', lambda h: h.tensor_tensor(NUM[:, 0:256], npss[:, 0:256], DEN[:, 0:256], ALU.mult), ['gps1', 'DEN'], ['NUM'])
            P.op('dve', lambda h: h.tensor_tensor(yb[:, 0:256], NUM[:, 0:256], gbTs[:], ALU.mult), ['NUM', 'gbTs'], ['yb'])
            for r_ in range(4):
                P.dma('pool', lambda h, r_=r_: h.dma_start(out=D['mp0_1_%d' % r_][:, SEG:SEGT],
                                                         in_=yb[:, 64 * r_:64 * r_ + 64]), ['yb'], ['mp0_1_%d' % r_], 'ybs%d' % r_)

        if stage >= 3:
            P.acc = acc
            all_gather_mix(P, D, 0)
            out_proj(P, D, 0, 'w_out0', 'x', 'x1', False, 0, NUM, 'NUMx')
            P.dma('sp', lambda h: h.dma_start(out=D['x1'][T:T + 64, :], in_=D['x'][T:T + 64, :]), [], ['x1'], 'x1pad')
        P.emit()


def build_layer1(nc, D, stage):
    with ExitStack() as es:
        P = Prog(nc, es, 'l1_')
        W = P.tile('W', [128, 8, 1024], BF16)
        g_rep = P.tile('g_rep', [128, 1, 1024], F32)
        ident = P.tile('ident', [128, 128], BF16)
        qcT = P.tile('qcT', [128, SEQ], BF16)
        kcT = P.tile('kcT', [128, SEQ], BF16)
        gcT = P.tile('gcT', [128, SEQ], BF16)
        Vtok = P.tile('Vtok', [128, 64, 129], BF16)
        qcTs = P.tile('qcTs', [128, 256], BF16)
        kcTs = P.tile('kcTs', [128, 256], BF16)
        vcTs = P.tile('vcTs', [128, 256], BF16)
        gcTs = P.tile('gcTs', [128, 256], BF16)
        vct = P.tile('vct', [128, 512], BF16)
        xt = P.tile('xt', [128, 2, 1024], F32)
        xn = P.tile('xn', [128, 1024], BF16)
        junk = P.tile('junk', [128, 1024], BF16)
        ss = P.tile('ss', [128, 4], F32)
        rstd = P.tile('rstd', [128, 4], F32)
        hnT = P.tile('hnT', [128, 8, 512], BF16)
        kst = P.tile('kst', [128, 512], F32)
        vst = P.tile('vst', [128, 512], F32)
        zer = P.tile('zer', [128, 2112], BF16)
        tp = P.psum('tp', [128, 8, 128], BF16)
        acc = [P.psum('acc%d' % i, [128, 512], F32) for i in range(3)]
        HT = {nm: P.tile('h_' + nm, [128, 512], F32) for nm in ('f', 'kk', 'bc', 'ebc', 'enbc', 'qf', 'kt', 'sq')}
        HB = {nm: P.tile('h_' + nm, [128, 512], BF16) for nm in ('qb', 'kb', 'k2b', 'vvT', 'sgd', 'yd')}
        vvtokc = P.tile('vvtokc', [128, 8, 128], BF16)
        k2tokc = P.tile('k2tokc', [128, 8, 128], BF16)
        vvtoks = P.tile('vvtoks', [128, 16, 128], BF16)
        k2toks = P.tile('k2toks', [128, 16, 128], BF16)
        attb = P.tile('attb', [128, 64], BF16)
        attbs = P.tile('attbs', [128, 4], BF16)
        Sst = P.tile('Sst', [128, 128], F32)
        Sbf = P.tile('Sbf', [128, 128], BF16)
        Ssm = [P.tile('Ssm%d' % i, [128, 128], F32) for i in range(2)]
        Ssb = P.tile('Ssb', [128, 128], BF16)
        maskT = P.tile('maskT', [128, 64], F32)
        onesf = P.tile('onesf', [128, 128], F32)
        ones64 = P.tile('ones64', [128, 64], F32)
        lbd = P.tile('lbd', [128, 2], F32)
        lbv = P.tile('lbv', [128, 2], F32)
        gnd = P.tile('gnd', [128, 1], F32)
        EC = P.tile('EC', [128, 2, 16, 128], F32)
        maskC = P.tile('maskC', [128, 128], F32)
        b31 = P.tile('b31', [128, 2], F32)
        lamc = P.tile('lamc', [128, 4, 64], F32)
        lamt = P.tile('lamt', [128, 64], F32)
        lamv = P.tile('lamv', [128, 4], F32)
        subln = P.tile('subln', [128, 1], F32)
        pFc = [P.tile('pFc%d' % i, [128, 512], F32) for i in range(2)]
        pTc = [P.tile('pTc%d' % i, [128, 512], BF16) for i in range(2)]
        e0 = P.tile('e0', [128, 128], F32)
        e1 = P.tile('e1', [128, 128], F32)
        ejunk = P.tile('ejunk', [128, 128], BF16)
        ewn = P.tile('ewn', [128, 128], BF16)
        est = P.tile('est', [128, 4], F32)
        ycb = P.tile('ycb', [128, 512], BF16)
        attps = P.psum('attps', [128, 64], F32)
        ups = P.psum('ups', [128, 512], F32)
        ops = P.psum('ops', [128, 512], F32)
        ssps = P.psum('ssps', [128, 512], F32)
        P.dma('pool', lambda h: h.dma_start(out=W[:], in_=D['w_in1'].rearrange("(c p) f -> p c f", p=128)), [], ['W'], 'W')
        P.dma('pool', lambda h: h.dma_start(out=ident[:], in_=D['ident']), [], ['ident'], 'ident')
        P.dma('sp', lambda h: h.dma_start(out=g_rep[:, 0:1, :], in_=D['ng'][1:2, :].partition_broadcast(128)), [], ['g_rep'], 'g_rep')
        P.op('dve', lambda h: h.memset(zer[:], 0.0), [], ['zer'])
        P.dma('sp', lambda h: h.dma_start(out=maskT[:], in_=D['maskT']), [], ['maskT'], 'maskT')
        P.dma('sp', lambda h: h.dma_start(out=lbd[:], in_=D['lbd']), [], ['lbd'], 'lbd')
        P.dma('sp', lambda h: h.dma_start(out=gnd[:], in_=D['gn_d']), [], ['gnd'], 'gnd')
        P.op('dve', lambda h: h.memset(onesf[:], 1.0), [], ['onesf'])
        P.op('dve', lambda h: h.memset(ones64[:], 1.0), [], ['ones64'])
        for tl, tn in ((vvtokc, 'vvtokc'), (k2tokc, 'k2tokc'), (vvtoks, 'vvtoks'), (k2toks, 'k2toks'), (attb, 'attb'), (attbs, 'attbs'), (Sst, 'Sst'), (Sbf, 'Sbf')):
            P.op('dve', lambda h, tl=tl: h.memset(tl[:], 0.0), [], [tn])
        P.op('act', lambda h: h.activation(out=lbd[:], in_=lbd[:], func=AF.Exp), ['lbd'], ['lbd'])
        P.op('dve', lambda h: h.tensor_tensor(lbv[:, 1:2], lbd[:, 0:1], lbd[:, 1:2], ALU.add), ['lbd'], ['lbv'])
        P.op('dve', lambda h: h.reciprocal(lbv[:, 1:2], lbv[:, 1:2]), ['lbv'], ['lbv'])
        P.op('dve', lambda h: h.tensor_tensor(lbv[:, 0:1], lbd[:, 1:2], lbv[:, 1:2], ALU.mult), ['lbd', 'lbv'], ['lbv'])
        P.op('dve', lambda h: h.tensor_scalar(lbv[:, 1:2], lbv[:, 0:1], -1.0, 1.0, ALU.mult, ALU.add), ['lbv'], ['lbv'])

        P.op('dve', lambda h: h.memset(Vtok[:], 1.0), [], ['Vtok'])
        P.dma('sp', lambda h: h.dma_start(out=EC[:], in_=D['biasC']), [], ['EC'], 'EC')
        P.dma('sp', lambda h: h.dma_start(out=maskC[:], in_=D['maskC']), [], ['maskC'], 'maskC')
        P.dma('sp', lambda h: h.dma_start(out=b31[:], in_=D['b31']), [], ['b31'], 'b31')
        P.dma('sp', lambda h: h.dma_start(out=lamc[:], in_=D['lamc']), [], ['lamc'], 'lamc')
        P.dma('sp', lambda h: h.dma_start(out=subln[:], in_=D['subln']), [], ['subln'], 'subln')
        P.op('act', lambda h: h.activation(out=EC[:], in_=EC[:], func=AF.Exp), ['EC'], ['EC'])
        for c_ in range(2):
            P.op('dve', lambda h, c_=c_: h.tensor_tensor(EC[:, c_, 0, :], EC[:, c_, 0, :], maskC[:], ALU.mult), ['EC', 'maskC'], ['EC'])
        for k_ in range(2):
            P.op('dve', lambda h, k_=k_: h.tensor_tensor(lamt[:], lamc[:, 2 * k_, :], lamc[:, 2 * k_ + 1, :], ALU.mult), ['lamc'], ['lamt'])
            P.op('dve', lambda h, k_=k_: h.reduce_sum(lamv[:, k_:k_ + 1], lamt[:], AX.X), ['lamt'], ['lamv'])
        P.op('act', lambda h: h.activation(out=lamv[:, 0:2], in_=lamv[:, 0:2], func=AF.Exp), ['lamv'], ['lamv'])
        P.op('dve', lambda h: h.tensor_tensor(lamv[:, 2:3], lamv[:, 1:2], lamv[:, 0:1], ALU.subtract), ['lamv'], ['lamv'])
        P.op('dve', lambda h: h.tensor_scalar(lamv[:, 2:3], lamv[:, 2:3], -LAM_INIT, None, ALU.add), ['lamv'], ['lamv'])
        if os.environ.get('K_NOC'):
            for r_ in range(4):
                P.dma('pool', lambda h, r_=r_: h.dma_start(out=D['mp1_0_%d' % r_], in_=zer[:]), ['zer'], ['mp1_0_%d' % r_], 'zer')

        def do_tile(r, kind, col0, ntok, lc0):
            nmm = max(ntok, 128)
            rmsnorm_transpose(P, D['x1'], col0, nmm, xt, ss, rstd, xn, junk, g_rep, ident, tp, hnT, 'xt', 0)
            n = ntok
            for fc in range(8):
                A = acc[fc % 3]
                an = 'acc%d' % (fc % 3)
                for kc in range(8):
                    P.op('pe', lambda h, A=A, kc=kc, fc=fc: h.matmul(A[:, 0:nmm], W[:, kc, fc * 128:(fc + 1) * 128],
                                                                    hnT[:, kc, 0:nmm], start=(kc == 0), stop=(kc == 7)),
                         ['W', 'hnT'], [an])
                if fc >= 4:
                    if fc == 4:
                        P.op('act', lambda h, A=A: h.activation(out=HT['qf'][:, 0:n], in_=A[:, 0:n], func=AF.Silu), [an], ['h_qf'])
                    elif fc == 5:
                        P.op('act', lambda h, A=A: h.activation(out=HT['f'][:, 0:n], in_=A[:, 0:n], func=AF.Sigmoid), [an], ['h_f'])
                    elif fc == 6:
                        P.op('act', lambda h, A=A: h.copy(HB['vvT'][:, 0:n], A[:, 0:n]), [an], ['h_vvT'])
                    else:
                        P.op('act', lambda h, A=A: h.activation(out=HB['sgd'][:, 0:n], in_=A[:, 0:n], func=AF.Silu), [an], ['h_sgd'])
                    continue
                if kind == 'p':
                    dq, dk, dg = qcT[:, lc0:lc0 + n], kcT[:, lc0:lc0 + n], gcT[:, lc0:lc0 + n]
                    dv = vct[:, 0:n]
                    names = ('qcT', 'kcT', 'vct', 'gcT')
                    ti = lc0 // 512
                else:
                    dq, dk, dv, dg = qcTs[:, lc0:lc0 + n], kcTs[:, lc0:lc0 + n], vcTs[:, lc0:lc0 + n], gcTs[:, lc0:lc0 + n]
                    names = ('qcTs', 'kcTs', 'vcTs', 'gcTs')
                    ti = r
                sfx = '_p' if kind == 'p' else '_s'
                if fc == 0:
                    P.op('act', lambda h, A=A: h.activation(out=dq, in_=A[:, 0:n], func=AF.Copy, scale=0.125), [an], [names[0]])
                elif fc in (1, 2):
                    dst = dk if fc == 1 else dv
                    st = kst if fc == 1 else vst
                    stn = 'kst' if fc == 1 else 'vst'
                    outk = 'kc' if fc == 1 else 'vc'
                    P.op('dve', lambda h, A=A, st=st: h.tensor_copy(st[:, 0:n], A[:, 0:n]), [an], [stn])
                    P.op('act', lambda h, st=st, dst=dst: h.copy(dst, st[:, 0:n]), [stn], [names[fc]])
                    P.dma('sp', lambda h, st=st, outk=outk: h.dma_start(out=D[outk + sfx][ti], in_=st[:, 0:n]), [stn], [], stn)
                    if fc == 2 and kind == 'p':
                        for j in range(4):
                            P.op('pe', lambda h, j=j: h.transpose(tp[:, j, :], vct[:, j * 128:(j + 1) * 128], ident[:]), ['vct', 'ident'], ['tp'])
                        P.op('act', lambda h: h.copy(Vtok[:, ti * 4:ti * 4 + 4, 0:128], tp[:, 0:4, :]), ['tp'], ['Vtok'])
                else:
                    P.op('act', lambda h, A=A: h.activation(out=dg, in_=A[:, 0:n], func=AF.Silu), [an], [names[3]])


            C = 64 if kind == 'p' else 4
            nch = n // C
            f_, kk_, bc_, ebc_, enbc_, qf_, kt_, sq_ = [HT[k_][:, 0:n] for k_ in ('f', 'kk', 'bc', 'ebc', 'enbc', 'qf', 'kt', 'sq')]
            qb_, kb_, k2b_, vvT_, sgd_, yd_ = [HB[k_][:, 0:n] for k_ in ('qb', 'kb', 'k2b', 'vvT', 'sgd', 'yd')]
            P.op('dve', lambda h: h.tensor_scalar(f_, f_, lbv[:, 1:2], lbv[:, 0:1], ALU.mult, ALU.add), ['h_f', 'lbv'], ['h_f'])
            P.op('dve', lambda h: h.tensor_scalar(kk_, f_, -1.0, 1.0, ALU.mult, ALU.add), ['h_f'], ['h_kk'])
            P.op('act', lambda h: h.activation(out=f_, in_=f_, func=AF.Ln), ['h_f'], ['h_f'])
            for c_ in range(nch):
                P.op('dve', lambda h, c_=c_: h.tensor_tensor_scan(bc_[:, c_ * C:(c_ + 1) * C], ones64[:, 0:C], f_[:, c_ * C:(c_ + 1) * C],
                                                                 0.0, ALU.mult, ALU.add), ['h_f', 'ones64'], ['h_bc'])
            P.op('act', lambda h: h.activation(out=ebc_, in_=bc_, func=AF.Exp), ['h_bc'], ['h_ebc'])
            P.op('act', lambda h: h.activation(out=enbc_, in_=bc_, func=AF.Exp, scale=-1.0), ['h_bc'], ['h_enbc'])
            P.op('dve', lambda h: h.scalar_tensor_tensor(qb_, qf_, 128.0 ** -0.5, ebc_, ALU.mult, ALU.mult), ['h_qf', 'h_ebc'], ['h_qb'])
            P.op('dve', lambda h: h.tensor_tensor(kt_, kk_, enbc_, ALU.mult), ['h_kk', 'h_enbc'], ['h_kt'])
            P.op('act', lambda h: h.copy(kb_, kt_), ['h_kt'], ['h_kb'])
            for c_ in range(nch):
                P.op('dve', lambda h, c_=c_: h.tensor_scalar(k2b_[:, c_ * C:(c_ + 1) * C], kt_[:, c_ * C:(c_ + 1) * C],
                                                            ebc_[:, (c_ + 1) * C - 1:(c_ + 1) * C], None, ALU.mult),
                     ['h_kt', 'h_ebc'], ['h_k2b'])
            vtk, k2k = (vvtokc, k2tokc) if kind == 'p' else (vvtoks, k2toks)
            vtn, k2n = ('vvtokc', 'k2tokc') if kind == 'p' else ('vvtoks', 'k2toks')
            ab, abn = (attb, 'attb') if kind == 'p' else (attbs, 'attbs')
            for (src, srcn, dstt, dstn) in ((vvT_, 'h_vvT', vtk, vtn), (k2b_, 'h_k2b', k2k, k2n)):
                for c0_ in range(0, nch, 8):
                    for c_ in range(c0_, min(nch, c0_ + 8)):
                        P.op('pe', lambda h, c_=c_, c0_=c0_, src=src: h.transpose(tp[0:C, c_ - c0_, :], src[:, c_ * C:(c_ + 1) * C], ident[:]),
                             [srcn, 'ident'], ['tp'])
                    P.op('act', lambda h, c0_=c0_, dstt=dstt: h.copy(dstt[0:C, c0_:min(nch, c0_ + 8), :], tp[0:C, 0:min(nch, c0_ + 8) - c0_, :]),
                         ['tp'], [dstn])
            for c_ in range(nch):
                cs = slice(c_ * C, (c_ + 1) * C)
                if kind == 'p':
                    Sf, Sfn, Sb, Sbn = Sst, 'Sst', Sbf, 'Sbf'
                else:
                    si = 16 * r + c_
                    Sf, Sfn, Sb, Sbn = Ssm[si % 2], 'Ssm%d' % (si % 2), Ssb, 'Ssb'
                    P.dma('sp', lambda h, Sf=Sf, si=si: h.dma_start(out=Sf[:], in_=D['s0'][si]), [], [Sfn], Sfn + 'l')
                    P.op('act', lambda h, Sf=Sf, Sb=Sb: h.copy(Sb[:], Sf[:]), [Sfn], [Sbn])
                P.op('pe', lambda h, cs=cs: h.matmul(attps[0:C, 0:C], kb_[:, cs], qb_[:, cs], start=True, stop=True), ['h_kb', 'h_qb'], ['attps'])
                P.op('dve', lambda h: h.tensor_tensor(ab[0:C, 0:C], attps[0:C, 0:C], maskT[0:C, 0:C], ALU.mult), ['attps', 'maskT'], [abn])
                P.op('pe', lambda h, c_=c_, cs=cs: h.matmul(ops[:, cs], vtk[:, c_, :], ab[:, 0:C], start=True, stop=False), [vtn, abn], ['ops'])
                P.op('pe', lambda h, cs=cs, Sb=Sb: h.matmul(ops[:, cs], Sb[:], qb_[:, cs], start=False, stop=True), [Sbn, 'h_qb'], ['ops'])
                P.op('pe', lambda h, c_=c_: h.matmul(ups[:, 0:128], k2k[:, c_, :], vtk[:, c_, :], start=True, stop=True), [k2n, vtn], ['ups'])
                P.op('dve', lambda h, c_=c_, Sf=Sf: h.scalar_tensor_tensor(Sf[:], Sf[:], ebc_[:, (c_ + 1) * C - 1:(c_ + 1) * C], ups[:, 0:128], ALU.mult, ALU.add),
                     [Sfn, 'h_ebc', 'ups'], [Sfn])
                if kind == 'p':
                    P.op('act', lambda h: h.copy(Sbf[:], Sst[:]), ['Sst'], ['Sbf'])
                else:
                    P.dma('pool', lambda h, Sf=Sf, si=si: h.dma_start(out=D['s_s'][si], in_=Sf[:]), [Sfn], [], Sfn + 's')
            if kind == 'p' and lc0 + n == SEQ:
                P.dma('pool', lambda h: h.dma_start(out=D['s_p'], in_=Sst[:]), ['Sst'], [], 'Sst_o')
            P.op('act', lambda h: h.activation(out=sq_, in_=ops[:, 0:n], func=AF.Square), ['ops'], ['h_sq'])
            P.op('pe', lambda h: h.matmul(ssps[:, 0:n], onesf[:], sq_, start=True, stop=True), ['onesf', 'h_sq'], ['ssps'])
            P.op('dve', lambda h: h.tensor_scalar(sq_, ssps[:, 0:n], 1.0 / 128, EPS, ALU.mult, ALU.add), ['ssps'], ['h_sq'])
            P.op('act', lambda h: h.activation(out=sq_, in_=sq_, func=AF.Sqrt), ['h_sq'], ['h_sq'])
            P.op('dve', lambda h: h.reciprocal(sq_, sq_), ['h_sq'], ['h_sq'])
            P.op('dve', lambda h: h.tensor_tensor(sq_, sq_, ops[:, 0:n], ALU.mult), ['h_sq', 'ops'], ['h_sq'])
            P.op('dve', lambda h: h.scalar_tensor_tensor(yd_, sq_, gnd[:, 0:1], sgd_, ALU.mult, ALU.mult), ['h_sq', 'gnd', 'h_sgd'], ['h_yd'])
            cc0 = col0 - r * SEGT
            P.dma('pool', lambda h: h.dma_start(out=D['mp1_1_%d' % r][:, cc0:cc0 + n], in_=yd_), ['h_yd'], ['mp1_1_%d' % r], 'h_yd')

        lim = int(os.environ.get('K_NT', '99'))
        for tt in token_tiles()[:lim]:
            do_tile(*tt)

        stc = (acc[0], acc[1])
        stn = ('acc0', 'acc1')
        Ot = ((acc[2], ops), (ssps, ups))
        Otn = (('acc2', 'ops'), ('ssps', 'ups'))

        def Oview(c_, j):
            return Ot[c_][j // 2][:, 0:258].rearrange("p (j v) -> p j v", j=2)[:, j % 2, :]

        def diff_qtile(qt):
            started = set()

            def kb_body(kb):
                jmin = max(0, kb - 4 * qt)
                qs = 512 * qt + 128 * jmin
                nq = 512 - 128 * jmin
                nsub = nq // 128
                d0 = 4 * qt + jmin - kb
                for c_ in range(2):
                    P.op('pe', lambda h, c_=c_: h.matmul(stc[c_][:, 0:nq], kcT[64 * c_:64 * c_ + 64, kb * 128:(kb + 1) * 128],
                                                         qcT[64 * c_:64 * c_ + 64, qs:qs + nq], start=True, stop=True),
                         ['kcT', 'qcT'], [stn[c_]])
                for c_ in range(2):
                    if d0 >= 13:
                        P.op('act', lambda h, c_=c_: h.activation(out=pTc[c_][:, 0:nq], in_=stc[c_][:, 0:nq], func=AF.Exp, bias=b31[:, c_:c_ + 1]),
                             [stn[c_], 'b31'], ['pTc%d' % c_])
                    else:
                        P.op('act', lambda h, c_=c_: h.activation(out=pFc[c_][:, 0:nq], in_=stc[c_][:, 0:nq], func=AF.Exp), [stn[c_]], ['pFc%d' % c_])
                        P.op('dve', lambda h, c_=c_: h.tensor_tensor(pTc[c_][:, 0:nq].rearrange("p (j q) -> p j q", q=128), pFc[c_][:, 0:nq].rearrange("p (j q) -> p j q", q=128),
                                                                    EC[:, c_, d0:d0 + nsub, :], ALU.mult),
                             ['pFc%d' % c_, 'EC'], ['pTc%d' % c_])
                for c_ in range(2):
                    for j in range(jmin, 4):
                        key = (c_, j // 2)
                        st_ = key not in started
                        started.add(key)
                        P.op('pe', lambda h, c_=c_, j=j, st_=st_: h.matmul(Oview(c_, j), pTc[c_][:, (j - jmin) * 128:(j - jmin + 1) * 128], Vtok[:, kb, :],
                                                                          start=st_, stop=(kb == 4 * qt + j), skip_group_check=True),
                             ['pTc%d' % c_, 'Vtok'], [Otn[c_][j // 2]])
            for kb in range(4 * qt + 4):
                kb_body(kb)

            def epi(j):
                O0, O1 = Oview(0, j), Oview(1, j)
                P.op('dve', lambda h: h.reciprocal(est[:, 0:1], O0[:, 128:129]), [Otn[0][j // 2]], ['est'])
                P.op('dve', lambda h: h.reciprocal(est[:, 1:2], O1[:, 128:129]), [Otn[1][j // 2]], ['est'])
                P.op('dve', lambda h: h.tensor_scalar(e0[:], O0[:, 0:128], est[:, 0:1], None, ALU.mult), [Otn[0][j // 2], 'est'], ['e0'])
                P.op('dve', lambda h: h.tensor_scalar(e1[:], O1[:, 0:128], est[:, 1:2], None, ALU.mult), [Otn[1][j // 2], 'est'], ['e1'])
                P.op('dve', lambda h: h.scalar_tensor_tensor(e0[:], e1[:], lamv[:, 2:3], e0[:], ALU.mult, ALU.add), ['e0', 'e1', 'lamv'], ['e0'])
                P.op('act', lambda h: h.activation(out=ejunk[:], in_=e0[:], func=AF.Square, accum_out=est[:, 2:3]), ['e0'], ['ejunk', 'est'])
                P.op('dve', lambda h: h.tensor_scalar(est[:, 3:4], est[:, 2:3], 1.0 / 128, EPS, ALU.mult, ALU.add), ['est'], ['est'])
                P.op('act', lambda h: h.activation(out=est[:, 3:4], in_=est[:, 3:4], func=AF.Sqrt), ['est'], ['est'])
                P.op('dve', lambda h: h.reciprocal(est[:, 3:4], est[:, 3:4]), ['est'], ['est'])
                P.op('dve', lambda h: h.tensor_scalar(ewn[:], e0[:], est[:, 3:4], 1.0 - LAM_INIT, ALU.mult, ALU.mult), ['e0', 'est'], ['ewn'])
                P.op('pe', lambda h, j=j: h.transpose(tp[:, j, :], ewn[:], ident[:]), ['ewn', 'ident'], ['tp'])

            for j in range(4):
                epi(j)
            P.op('dve', lambda h: h.scalar_tensor_tensor(ycb[:].rearrange("p (j q) -> p j q", q=128), tp[:, 0:4, :], subln[:, 0:1],
                                                         gcT[:, 512 * qt:512 * qt + 512].rearrange("p (j q) -> p j q", q=128), ALU.mult, ALU.mult),
                 ['tp', 'subln', 'gcT'], ['ycb'])
            r_ = qt // 4
            cc0 = (qt % 4) * 512
            P.dma('pool', lambda h: h.dma_start(out=D['mp1_0_%d' % r_][:, cc0:cc0 + 512], in_=ycb[:]), ['ycb'], ['mp1_0_%d' % r_], 'ycb')

        if not os.environ.get('K_NOC'):
            nqt = int(os.environ.get('K_NQT', '16'))
            for qt in range(nqt):
                diff_qtile(qt)
        for (tl, tn, dn) in ((qcTs, 'qcTs', 'qcs'), (kcTs, 'kcTs', 'kcs'), (vcTs, 'vcTs', 'vcs'), (gcTs, 'gcTs', 'gcs')):
            P.dma('pool', lambda h, tl=tl, dn=dn: h.dma_start(out=D[dn], in_=tl[:]), [tn], [dn], 'st_' + dn)
        P.emit()

    with ExitStack() as es:
        P = Prog(nc, es, 'l1d_')
        ident = P.tile('ident', [128, 128], BF16)
        kv = [P.tile('kv%d' % i, [128, 17, 257], BF16) for i in range(2)]
        qs = P.tile('qs', [128, 256], BF16)
        ks = P.tile('ks', [128, 256], BF16)
        vs = P.tile('vs', [128, 256], BF16)
        gs = P.tile('gs', [128, 256], BF16)
        ECs = P.tile('ECs', [128, 17, 2, 4], F32)
        mCs = P.tile('mCs', [128, 17, 2, 4], F32)
        pFd = [P.tile('pFd%d' % i, [128, 17, 4], F32) for i in range(2)]
        pTp = [P.tile('pTp%d' % i, [128, 17, 128], BF16) for i in range(2)]
        ptb = P.tile('ptb', [128, 1, 1024], I32)
        idx = P.tile('idx', [128, 1024], I32)
        iot = P.tile('iot', [128, 1], F32)
        lamc = P.tile('lamc', [128, 4, 64], F32)
        lamt = P.tile('lamt', [128, 64], F32)
        lamv = P.tile('lamv', [128, 4], F32)
        subln = P.tile('subln', [128, 1], F32)
        e0 = P.tile('e0', [128, 128], F32)
        e1 = P.tile('e1', [128, 128], F32)
        ejunk = P.tile('ejunk', [128, 128], BF16)
        ewn = P.tile('ewn', [128, 128], BF16)
        est = P.tile('est', [128, 4], F32)
        ycs = P.tile('ycs', [128, 128], BF16)
        tp = P.psum('tp', [128, 8, 128], BF16)
        std = [P.psum('std%d' % i, [128, 17, 4], F32) for i in range(2)]
        Od = [P.psum('Od%d' % i, [128, 129], F32) for i in range(2)]
        P.dma('pool', lambda h: h.dma_start(out=ident[:], in_=D['ident']), [], ['ident'], 'ident')
        for (tl, tn, dn) in ((qs, 'qs', 'qcs'), (ks, 'ks', 'kcs'), (vs, 'vs', 'vcs'), (gs, 'gs', 'gcs')):
            P.dma('sp', lambda h, tl=tl, dn=dn: h.dma_start(out=tl[:], in_=D[dn]), [dn], [tn], tn)
        P.dma('sp', lambda h: h.dma_start(out=ECs[:], in_=D['biasCs']), [], ['ECs'], 'ECs')
        P.dma('sp', lambda h: h.dma_start(out=mCs[:], in_=D['multCs']), [], ['mCs'], 'mCs')
        P.dma('sp', lambda h: h.dma_start(out=iot[:], in_=D['iota']), [], ['iot'], 'iot')
        P.dma('sp', lambda h: h.dma_start(out=lamc[:], in_=D['lamc']), [], ['lamc'], 'lamc')
        P.dma('sp', lambda h: h.dma_start(out=subln[:], in_=D['subln']), [], ['subln'], 'subln')
        P.dma('sp', lambda h: h.dma_start(out=ptb[:], in_=D['pt'].partition_broadcast(128)), [], ['ptb'], 'ptb')
        P.op('act', lambda h: h.activation(out=ECs[:], in_=ECs[:], func=AF.Exp), ['ECs'], ['ECs'])
        P.op('dve', lambda h: h.tensor_tensor(ECs[:], ECs[:], mCs[:], ALU.mult), ['ECs', 'mCs'], ['ECs'])
        P.op('dve', lambda h: h.tensor_scalar(idx[:], ptb[:, 0, :], 128.0, iot[:, 0:1], ALU.mult, ALU.add), ['ptb', 'iot'], ['idx'])
        for k_ in range(2):
            P.op('dve', lambda h, k_=k_: h.tensor_tensor(lamt[:], lamc[:, 2 * k_, :], lamc[:, 2 * k_ + 1, :], ALU.mult), ['lamc'], ['lamt'])
            P.op('dve', lambda h, k_=k_: h.reduce_sum(lamv[:, k_:k_ + 1], lamt[:], AX.X), ['lamt'], ['lamv'])
        P.op('act', lambda h: h.activation(out=lamv[:, 0:2], in_=lamv[:, 0:2], func=AF.Exp), ['lamv'], ['lamv'])
        P.op('dve', lambda h: h.tensor_tensor(lamv[:, 2:3], lamv[:, 1:2], lamv[:, 0:1], ALU.subtract), ['lamv'], ['lamv'])
        P.op('dve', lambda h: h.tensor_scalar(lamv[:, 2:3], lamv[:, 2:3], -LAM_INIT, None, ALU.add), ['lamv'], ['lamv'])
        for i_ in range(2):
            P.op('dve', lambda h, i_=i_: h.memset(kv[i_][:, :, 0:256], 0.0), [], ['kv%d_%d' % (i_, j) for j in range(17)])
            P.op('dve', lambda h, i_=i_: h.memset(kv[i_][:, :, 256:257], 1.0), [], ['kv%d_%d' % (i_, j) for j in range(17)])

        def dec_sample(s_):
            i = s_ % 2
            u = s_ % 32
            c4 = slice(4 * s_, 4 * s_ + 4)
            for j in range(16):
                n_ = 16 * s_ + j
                P.dma('pool', lambda h, j=j, n_=n_: h.indirect_dma_start(
                    out=kv[i][:, j, 0:256], out_offset=None, in_=D['kvpool'],
                    in_offset=bass.IndirectOffsetOnAxis(ap=idx[:, n_:n_ + 1], axis=0)),
                    ['idx'], ['kv%d_%d' % (i, j)], 'kvg%d_%d' % (i, j))
            P.op('dve', lambda h: h.tensor_copy(kv[i][:, 16, 0:4], ks[:, c4]), ['ks'], ['kv%d_16' % i])
            P.op('pe', lambda h: h.transpose(tp[0:4, 5, :], vs[:, c4], ident[:]), ['vs', 'ident'], ['tp'])
            P.op('act', lambda h: h.copy(kv[i][0:4, 16, 128:256], tp[0:4, 5, :]), ['tp'], ['kv%d_16' % i])
            for c_ in range(2):
                for j in range(17):
                    P.op('pe', lambda h, c_=c_, j=j: h.matmul(std[c_][:, j, :], kv[i][64 * c_:64 * c_ + 64, j, 0:128],
                                                             qs[64 * c_:64 * c_ + 64, c4], start=True, stop=True),
                         ['kv%d_%d' % (i, j), 'qs'], ['std%d' % c_])
            for c_ in range(2):
                P.op('act', lambda h, c_=c_: h.activation(out=pFd[c_][:], in_=std[c_][:], func=AF.Exp), ['std%d' % c_], ['pFd%d' % c_])
                if u == 0:
                    P.op('dve', lambda h, c_=c_: h.memset(pTp[c_][:], 0.0), [], ['pTp%d' % c_])
                else:
                    P.op('dve', lambda h, c_=c_: h.memset(pTp[c_][:, :, 4 * (u - 1):4 * u], 0.0), [], ['pTp%d' % c_])
                P.op('dve', lambda h, c_=c_: h.tensor_tensor(pTp[c_][:, :, 4 * u:4 * u + 4], pFd[c_][:], ECs[:, :, c_, :], ALU.mult),
                     ['pFd%d' % c_, 'ECs'], ['pTp%d' % c_])
            for c_ in range(2):
                for j in range(17):
                    P.op('pe', lambda h, c_=c_, j=j: h.matmul(Od[c_][:, 0:129], pTp[c_][:, j, :], kv[i][:, j, 128:257],
                                                             start=(u == 0 and j == 0), stop=(u == 31 and j == 16), skip_group_check=True),
                         ['pTp%d' % c_, 'kv%d_%d' % (i, j)], ['Od%d' % c_])

        def dec_epilogue(gr):
            O0, O1 = Od[0], Od[1]
            P.op('dve', lambda h: h.reciprocal(est[:, 0:1], O0[:, 128:129]), ['Od0'], ['est'])
            P.op('dve', lambda h: h.reciprocal(est[:, 1:2], O1[:, 128:129]), ['Od1'], ['est'])
            P.op('dve', lambda h: h.tensor_scalar(e0[:], O0[:, 0:128], est[:, 0:1], None, ALU.mult), ['Od0', 'est'], ['e0'])
            P.op('dve', lambda h: h.tensor_scalar(e1[:], O1[:, 0:128], est[:, 1:2], None, ALU.mult), ['Od1', 'est'], ['e1'])
            P.op('dve', lambda h: h.scalar_tensor_tensor(e0[:], e1[:], lamv[:, 2:3], e0[:], ALU.mult, ALU.add), ['e0', 'e1', 'lamv'], ['e0'])
            P.op('act', lambda h: h.activation(out=ejunk[:], in_=e0[:], func=AF.Square, accum_out=est[:, 2:3]), ['e0'], ['ejunk', 'est'])
            P.op('dve', lambda h: h.tensor_scalar(est[:, 3:4], est[:, 2:3], 1.0 / 128, EPS, ALU.mult, ALU.add), ['est'], ['est'])
            P.op('act', lambda h: h.activation(out=est[:, 3:4], in_=est[:, 3:4], func=AF.Sqrt), ['est'], ['est'])
            P.op('dve', lambda h: h.reciprocal(est[:, 3:4], est[:, 3:4]), ['est'], ['est'])
            P.op('dve', lambda h: h.tensor_scalar(ewn[:], e0[:], est[:, 3:4], 1.0 - LAM_INIT, ALU.mult, ALU.mult), ['e0', 'est'], ['ewn'])
            P.op('pe', lambda h: h.transpose(tp[:, 0, :], ewn[:], ident[:]), ['ewn', 'ident'], ['tp'])
            P.op('dve', lambda h: h.scalar_tensor_tensor(ycs[:], tp[:, 0, :], subln[:, 0:1], gs[:, 128 * gr:128 * gr + 128], ALU.mult, ALU.mult),
                 ['tp', 'subln', 'gs'], ['ycs'])
            for q_ in range(2):
                r_ = 2 * gr + q_
                P.dma('pool', lambda h, r_=r_, q_=q_: h.dma_start(out=D['mp1_0_%d' % r_][:, SEG:SEGT], in_=ycs[:, 64 * q_:64 * q_ + 64]),
                      ['ycs'], ['mp1_0_%d' % r_], 'ycs%d' % q_)

        for s_ in range(64):
            dec_sample(s_)
            if s_ % 32 == 31:
                dec_epilogue(s_ // 32)
        P.emit()
    with ExitStack() as es:
        P = Prog(nc, es, 'l1o_')
        xbuf = P.tile('xbuf', [128, 2048], F32)
        P.acc = [P.psum('acc%d' % i, [128, 512], F32) for i in range(2)]
        all_gather_mix(P, D, 1)
        out_proj(P, D, 1, 'w_out1', 'x1', 'y', True, 2, xbuf, 'xbuf')
        P.emit()


def build_nc(stage=99):
    nc = bass.Bass("TRN2", target_bir_lowering=False)
    D = {}

    def inp(name, shape, dt=F32):
        D[name] = nc.dram_tensor(name, shape, dt, kind="ExternalInput").ap()

    def outp(name, shape, dt=F32):
        D[name] = nc.dram_tensor(name, shape, dt, kind="ExternalOutput").ap()

    def internal(name, shape, dt):
        D[name] = nc.dram_tensor(name, shape, dt, kind="Internal").ap()

    inp('x', [T + 64, 1024])
    inp('w_in0', [1024, 768])
    inp('ng', [3, 1024])
    inp('ident', [128, 128])
    inp('vecA', [128, 8])
    inp('wr', [128, 128])
    inp('wi', [128, 128])
    inp('convs', [128, 64, 3])
    inp('h0s', [128, 64])
    inp('biasB', [128, 3, 2, 256])
    inp('maskB', [128, 256])
    inp('biasBs', [128, 8, 2, 4])
    inp('multBs', [128, 8, 2, 4])
    inp('kwin', [64, 128, 896])
    inp('vwin', [64, 896, 128])
    outp('conv_p', [128, 3])
    outp('h_p', [128, 1])
    outp('wk_p', [4, 128, 512])
    outp('wv_p', [4, 128, 512])
    outp('conv_s', [128, 64, 3])
    outp('h_s', [128, 64])
    outp('wk_s', [4, 128, 64])
    outp('wv_s', [4, 128, 64])
    for L in range(2):
        for hf in range(2):
            for r_ in range(4):
                internal('mp%d_%d_%d' % (L, hf, r_), [128, SEGT], BF16)
                internal('mg%d_%d_%d' % (L, hf, r_), [512, SEGT], BF16)
    if os.environ.get('K_X1OUT'):
        outp('x1', [T + 64, 1024])
    else:
        internal('x1', [T + 64, 1024], F32)
    inp('w_out0', [1024, 1024])
    inp('w_in1', [1024, 1024])
    inp('w_out1', [1024, 1024])
    outp('kc_p', [16, 128, 512])
    outp('vc_p', [16, 128, 512])
    outp('kc_s', [4, 128, 64])
    outp('vc_s', [4, 128, 64])
    outp('y', [T, 1024])
    outp('s_p', [128, 128])
    outp('s_s', [64, 128, 128])
    inp('s0', [64, 128, 128])
    inp('kvpool', [2560 * 128, 256])
    inp('pt', [1, 1024], I32)
    inp('iota', [128, 1])
    inp('biasCs', [128, 17, 2, 4])
    inp('multCs', [128, 17, 2, 4])
    internal('qcs', [128, 256], BF16)
    internal('kcs', [128, 256], BF16)
    internal('vcs', [128, 256], BF16)
    internal('gcs', [128, 256], BF16)
    inp('biasC', [128, 2, 16, 128])
    inp('maskC', [128, 128])
    inp('b31', [128, 2])
    inp('lamc', [128, 4, 64])
    inp('subln', [128, 1])
    inp('maskT', [128, 64])
    inp('lbd', [128, 2])
    inp('gn_d', [128, 1])
    build_layer0(nc, D, stage)
    if stage >= 4:
        build_layer1(nc, D, stage)
    return nc


def prep_inputs(I, c):
    b, g = c // 4, c % 4
    m = {}
    xs = I['x_sample'].reshape(128, 4, 1024)
    segs = []
    for r in range(4):
        segs.append(I['x_prompt'][b, SEG * r:SEG * (r + 1)])
        segs.append(xs[64 * b + 16 * r:64 * b + 16 * r + 16].reshape(64, 1024))
    segs.append(segs[0][0:64])
    m['x'] = np.ascontiguousarray(np.concatenate(segs, 0))
    w = I['w_in_ab'][0]
    m['w_in0'] = np.ascontiguousarray(np.concatenate([w[:, blk * 512 + 128 * g: blk * 512 + 128 * g + 128] for blk in range(6)], 1))
    m['ng'] = np.ascontiguousarray(np.stack([I['norm_g'][0], I['norm_g'][1], I['norm_final']], 0))
    m['ident'] = np.eye(128, dtype=np.float32)
    sl = slice(128 * g, 128 * g + 128)
    va = np.zeros((128, 8), np.float32)
    va[:, 0:4] = I['conv_w_a'][0][:, sl].T
    va[:, 4] = I['conv_b_a'][0][sl]
    va[:, 5] = I['b_r_a'][0].reshape(-1)[sl]
    va[:, 6] = I['b_i_a'][0].reshape(-1)[sl]
    va[:, 7] = I['lam_a'][0][sl]
    m['vecA'] = va
    for nm, key in (('wr', 'w_r_a'), ('wi', 'w_i_a')):
        bd = np.zeros((128, 128), np.float32)
        bd[0:64, 0:64] = I[key][0][2 * g]
        bd[64:128, 64:128] = I[key][0][2 * g + 1]
        m[nm] = bd
    m['convs'] = np.ascontiguousarray(I['state_conv_a'][0][64 * b:64 * b + 64][:, :, sl].transpose(2, 0, 1))
    m['h0s'] = np.ascontiguousarray(I['state_h_a'][0][64 * b:64 * b + 64][:, sl].T)
    prep_mixb(I, c, m)
    m['w_out0'] = np.ascontiguousarray(I['w_out_ab'][0])
    m['w_out1'] = np.ascontiguousarray(I['w_out_cd'][0])
    w1 = I['w_in_cd'][0]
    rb = I['rel_bias']
    k_ = np.arange(128)[:, None, None]
    dd = np.arange(16)[None, :, None]
    q_ = np.arange(128)[None, None, :]
    dist = dd * 128 + q_ - k_
    bk = t5_bucket(np.maximum(dist, 0))
    bc_ = np.zeros((128, 2, 16, 128), np.float32)
    for c_ in range(2):
        bc_[:, c_] = np.where(dist >= 0, rb[bk, c_ * 4 + g], 0.0)
    m['biasC'] = bc_
    m['maskC'] = (np.arange(128)[None, :] >= np.arange(128)[:, None]).astype(np.float32)
    m['b31'] = np.ascontiguousarray(np.broadcast_to(rb[31, [g, 4 + g]][None, :], (128, 2))).astype(np.float32)
    m['lamc'] = np.ascontiguousarray(np.broadcast_to(I['lam_c'][0][None], (128, 4, 64))).astype(np.float32)
    m['subln'] = np.ascontiguousarray(I['subln_c'][0].reshape(128, 1))
    kp = I['cache_k_c'][0][:, :, g, :].transpose(0, 2, 1)
    vp = I['cache_v_c'][0][:, :, g, :]
    m['kvpool'] = np.ascontiguousarray(np.concatenate([kp, vp], axis=2).reshape(2560 * 128, 256))
    m['pt'] = np.ascontiguousarray(I['page_table'][64 * b:64 * b + 64].reshape(1, 1024).astype(np.int32))
    m['iota'] = np.arange(128, dtype=np.float32).reshape(128, 1)
    p_ = np.arange(128)[:, None, None]
    j_ = np.arange(17)[None, :, None]
    t_ = np.arange(4)[None, None, :]
    kpos = np.where(j_ < 16, j_ * 128 + p_, 2048 + p_)
    distd = 2048 + t_ - kpos
    okd = (distd >= 0) & ((j_ < 16) | (p_ < 4))
    bkd = t5_bucket(np.maximum(distd, 0))
    bcs = np.zeros((128, 17, 2, 4), np.float32)
    for c_ in range(2):
        bcs[:, :, c_, :] = np.where(okd, rb[bkd, c_ * 4 + g], 0.0)
    m['biasCs'] = bcs
    m['multCs'] = np.ascontiguousarray(np.broadcast_to(okd[:, :, None, :], (128, 17, 2, 4))).astype(np.float32)
    m['s0'] = np.ascontiguousarray(I['state_s_d'][0][64 * b:64 * b + 64, g])
    mt = (np.arange(64)[:, None] <= np.arange(64)[None, :]).astype(np.float32)
    m['maskT'] = np.ascontiguousarray(np.concatenate([mt, mt], 0))
    m['lbd'] = np.ascontiguousarray(I['lb_d'][:, 128 * g:128 * g + 128].T)
    m['gn_d'] = np.ascontiguousarray(I['gnorm_d'][0].reshape(128, 1))
    m['w_in1'] = np.ascontiguousarray(np.concatenate([w1[:, blk * 512 + 128 * g: blk * 512 + 128 * g + 128] for blk in range(8)], 1))
    return m


def t5_bucket(dist):
    dist = np.asarray(dist)
    n = np.maximum(dist, 0)
    nf = np.maximum(n, 1).astype(np.float32)
    large = 16 + (np.log(nf / np.float32(16)) / np.float32(np.log(128.0)) * np.float32(16)).astype(np.int32)
    large = np.minimum(large, 31)
    return np.where(n < 16, n, large)


def win_rows():
    far = [16 * j + t for j in range(96) for t in range(4)]
    near = list(range(1536, 2048))
    return np.array(far + near, dtype=np.int64)


def prep_mixb(I, c, m):
    b, g = c // 4, c % 4
    rb = I['rel_bias']
    j = np.arange(128)[:, None]
    ip = np.arange(256)[None, :]
    rel = ip - j
    valid = (rel >= 0) & (rel <= 128)
    bb = np.zeros((128, 3, 2, 256), np.float32)
    for pat, dil in enumerate((1, 4, 16)):
        bk = t5_bucket(dil * np.clip(rel, 0, 128))
        for h in range(2):
            bb[:, pat, h, :] = np.where(valid, rb[bk, 2 * g + h], 0.0)
    m['biasB'] = bb
    m['maskB'] = valid.astype(np.float32)
    rows = np.concatenate([win_rows(), 2048 + np.arange(4), np.full(124, -1)])
    rows = rows.reshape(8, 128).T
    t = np.arange(4)[None, None, :]
    dist = 2048 + t - rows[:, :, None]
    ok = (rows[:, :, None] >= 0) & (dist >= 0)
    mult = np.zeros(dist.shape, np.float32)
    for dil in (1, 4, 16):
        mult += (ok & (dist % dil == 0) & (dist // dil <= 128)).astype(np.float32)
    bk = t5_bucket(np.where(ok, dist, 0))
    bs = np.zeros((128, 8, 2, 4), np.float32)
    for h in range(2):
        bs[:, :, h, :] = np.where(ok, rb[bk, 2 * g + h], 0.0)
    m['biasBs'] = bs
    m['multBs'] = np.ascontiguousarray(np.broadcast_to(mult[:, :, None, :], (128, 8, 2, 4)))
    wr_ = win_rows()
    ck = I['cache_win_k'][0][64 * b:64 * b + 64][:, wr_][:, :, 2 * g:2 * g + 2, :].reshape(64, 896, 128)
    cv = I['cache_win_v'][0][64 * b:64 * b + 64][:, wr_][:, :, 2 * g:2 * g + 2, :].reshape(64, 896, 128)
    m['kwin'] = np.ascontiguousarray(ck.transpose(0, 2, 1))
    m['vwin'] = np.ascontiguousarray(cv)


_NC_CACHE = {}


def run_device(I, stage=99):
    if stage not in _NC_CACHE:
        _NC_CACHE[stage] = build_nc(stage)
    nc = _NC_CACHE[stage]
    in_maps = [prep_inputs(I, c) for c in range(8)]
    res = run_bass_kernel_spmd(nc, in_maps, core_ids=list(range(8)))
    return res.results


def kernel(**I):
    I = {k: np.asarray(v) for k, v in I.items()}
    R = run_device(I)
    f32 = np.float32
    y_p = np.zeros((2, SEQ, 1024), f32)
    y_s = np.zeros((128, 4, 1024), f32)
    conv_p = np.zeros((1, 2, 3, 512), f32)
    h_p = np.zeros((1, 2, 512), f32)
    wk_p = np.zeros((1, 2, 2048, 8, 64), f32)
    wv_p = np.zeros((1, 2, 2048, 8, 64), f32)
    kc_p = np.zeros((1, 2, SEQ, 4, 128), f32)
    vc_p = np.zeros((1, 2, SEQ, 4, 128), f32)
    s_p = np.zeros((1, 2, 4, 128, 128), f32)
    conv_s = np.zeros((1, 128, 3, 512), f32)
    h_s = np.zeros((1, 128, 512), f32)
    wk_s = np.zeros((1, 128, 4, 8, 64), f32)
    wv_s = np.zeros((1, 128, 4, 8, 64), f32)
    kc_s = np.zeros((1, 128, 4, 4, 128), f32)
    vc_s = np.zeros((1, 128, 4, 4, 128), f32)
    s_s = np.zeros((1, 128, 4, 128, 128), f32)

    def fm(a):
        return a.transpose(1, 0, 2).reshape(128, -1).T

    for c in range(8):
        b, g = c // 4, c % 4
        r = R[c]
        sl = slice(128 * g, 128 * g + 128)
        ss_ = slice(64 * b, 64 * b + 64)
        if g == 0:
            y = r['y']
            for q in range(4):
                y_p[b, SEG * q:SEG * (q + 1)] = y[q * SEGT:q * SEGT + SEG]
                y_s[64 * b + 16 * q:64 * b + 16 * q + 16] = y[q * SEGT + SEG:(q + 1) * SEGT].reshape(16, 4, 1024)
        conv_p[0, b, :, sl] = r['conv_p'].T
        h_p[0, b, sl] = r['h_p'][:, 0]
        wk_p[0, b].reshape(2048, 512)[:, sl] = fm(r['wk_p'])
        wv_p[0, b].reshape(2048, 512)[:, sl] = fm(r['wv_p'])
        kc_p[0, b, :, g, :] = fm(r['kc_p'])
        vc_p[0, b, :, g, :] = fm(r['vc_p'])
        conv_s[0, ss_][:, :, sl] = r['conv_s'].transpose(1, 2, 0)
        h_s[0, ss_, sl] = r['h_s'].T
        wk_s[0, ss_].reshape(64, 4, 512)[:, :, sl] = fm(r['wk_s']).reshape(64, 4, 128)
        wv_s[0, ss_].reshape(64, 4, 512)[:, :, sl] = fm(r['wv_s']).reshape(64, 4, 128)
        kc_s[0, ss_, :, g, :] = fm(r['kc_s']).reshape(64, 4, 128)
        vc_s[0, ss_, :, g, :] = fm(r['vc_s']).reshape(64, 4, 128)
        if 's_p' in r:
            s_p[0, b, g] = r['s_p']
            s_s[0, ss_, g] = r['s_s']
    return (y_p, y_s, conv_p, h_p, wk_p, wv_p, kc_p, vc_p, s_p, conv_s, h_s, wk_s, wv_s, kc_s, vc_s, s_s)
```
